# Optimizing a Trainium2 kernel written in Bass

```python
import math
import jax, jax.numpy as jnp
from jax import lax
import numpy as np

D_MODEL = 2048
BATCH = 8
SEQ = 2048
DEPTH = 2

ATT_HEADS = 8
ATT_HEAD_DIM = 64
ATT_V_DIM = 2 * ATT_HEAD_DIM
ATT_QK_WIDTH = ATT_HEADS * 2 * ATT_HEAD_DIM
ATT_WIDTH = ATT_HEADS * ATT_V_DIM
LRU_WIDTH = D_MODEL // 2
LRU_BLOCKS = 8
LRU_BLOCK_W = LRU_WIDTH // LRU_BLOCKS
CONV_WIDTH = 4
LRU_C = 8.0
Q_BLOCK = 128
LN_EPS = 1e-5
SUBLN_EPS = 1e-5
DN_ALPHA = (2 * DEPTH) ** 0.25
DN_BETA = (8 * DEPTH) ** -0.25

IN_WIDTHS = [ATT_QK_WIDTH, ATT_QK_WIDTH, ATT_WIDTH, ATT_WIDTH,
             LRU_WIDTH, LRU_WIDTH, D_MODEL, D_MODEL]
IN_TOTAL = int(sum(IN_WIDTHS))
SPLIT_IDX = [int(v) for v in np.cumsum(IN_WIDTHS)[:-1]]

kernel_name = "hybrid_diffattn_rglru_deepnorm"


def _layer_norm(x, g, b):
    xf = x.astype(jnp.float32)
    mu = jnp.mean(xf, axis=-1, keepdims=True)
    var = jnp.mean(jnp.square(xf - mu), axis=-1, keepdims=True)
    y = (xf - mu) * lax.rsqrt(var + LN_EPS) * g.astype(jnp.float32) + b.astype(jnp.float32)
    return y.astype(x.dtype)


def _diff_attention(q, k, v, lam, lam_init, subln_g):
    B, S = q.shape[0], q.shape[1]
    nb = S // Q_BLOCK
    scale = ATT_HEAD_DIM ** -0.5
    slopes = jnp.exp2(-(8.0 / ATT_HEADS) * jnp.arange(1, ATT_HEADS + 1, dtype=jnp.float32))
    kpos = jnp.arange(S)
    vf = v.astype(jnp.float32)
    qb = q.reshape(B, nb, Q_BLOCK, ATT_HEADS, 2, ATT_HEAD_DIM).transpose(1, 0, 2, 3, 4, 5)

    def block(args):
        qi, i = args
        qpos = i * Q_BLOCK + jnp.arange(Q_BLOCK)
        dist = (qpos[:, None] - kpos[None, :]).astype(jnp.float32)
        bias = jnp.where(dist >= 0, -slopes[:, None, None] * dist, -jnp.inf)
        s = jnp.einsum('bqhcd,bkhcd->bchqk', qi, k,
                       preferred_element_type=jnp.float32) * scale + bias
        p = jax.nn.softmax(s, axis=-1)
        w = p[:, 0] - lam * p[:, 1]
        return jnp.einsum('bhqk,bkhe->bqhe', w, vf)

    o = lax.map(block, (qb, jnp.arange(nb)))
    o = o.transpose(1, 0, 2, 3, 4).reshape(B, S, ATT_HEADS, ATT_V_DIM)
    o = o * lax.rsqrt(jnp.mean(jnp.square(o), axis=-1, keepdims=True) + SUBLN_EPS)
    o = o * subln_g.astype(jnp.float32) * (1.0 - lam_init)
    return o.reshape(B, S, ATT_WIDTH).astype(q.dtype)


def _rg_lru(xb, conv_w, conv_b, w_r, b_r, w_i, b_i, lru_lambda):
    B, S, C = xb.shape
    xc = lax.conv_general_dilated(
        xb, conv_w[:, None, :], window_strides=(1,), padding=[(CONV_WIDTH - 1, 0)],
        dimension_numbers=('NWC', 'WIO', 'NWC'), feature_group_count=C) + conv_b
    xblk = xc.reshape(B, S, LRU_BLOCKS, LRU_BLOCK_W)
    r = jax.nn.sigmoid(jnp.einsum('bsni,nio->bsno', xblk, w_r).reshape(B, S, C) + b_r)
    ig = jax.nn.sigmoid(jnp.einsum('bsni,nio->bsno', xblk, w_i).reshape(B, S, C) + b_i)
    log_a = -LRU_C * r.astype(jnp.float32) * jax.nn.softplus(-lru_lambda.astype(jnp.float32))
    a = jnp.exp(log_a)
    u = jnp.sqrt(-jnp.expm1(2.0 * log_a)) * (ig * xc).astype(jnp.float32)

    def combine(left, right):
        a_l, b_l = left
        a_r, b_r = right
        return a_r * a_l, a_r * b_l + b_r

    _, h = lax.associative_scan(combine, (a, u), axis=1)
    return h.astype(xb.dtype)


def setup_inputs(seed: int = 0) -> dict:
    key = jax.random.key(seed)
    ks = jax.random.split(key, 24)
    L, D = DEPTH, D_MODEL
    col_scale = np.concatenate([
        np.ones(ATT_QK_WIDTH * 2, np.float32),
        np.full(ATT_WIDTH, DN_BETA, np.float32),
        np.ones(ATT_WIDTH, np.float32),
        np.full(LRU_WIDTH, DN_BETA, np.float32),
        np.ones(LRU_WIDTH + 2 * D, np.float32)])
    x = jax.random.normal(ks[0], (BATCH, SEQ, D), jnp.float32)
    w_in = jax.random.normal(ks[1], (L, D, IN_TOTAL), jnp.float32) * (D ** -0.5) * jnp.asarray(col_scale)
    conv_w = jax.random.normal(ks[2], (L, CONV_WIDTH, LRU_WIDTH), jnp.float32) * (CONV_WIDTH ** -0.5)
    conv_b = 0.01 * jax.random.normal(ks[3], (L, LRU_WIDTH), jnp.float32)
    w_rgate = jax.random.normal(ks[4], (L, LRU_BLOCKS, LRU_BLOCK_W, LRU_BLOCK_W), jnp.float32) * (LRU_BLOCK_W ** -0.5)
    b_rgate = 0.01 * jax.random.normal(ks[5], (L, LRU_WIDTH), jnp.float32)
    w_igate = jax.random.normal(ks[6], (L, LRU_BLOCKS, LRU_BLOCK_W, LRU_BLOCK_W), jnp.float32) * (LRU_BLOCK_W ** -0.5)
    b_igate = 0.01 * jax.random.normal(ks[7], (L, LRU_WIDTH), jnp.float32)
    a_pow = jax.random.uniform(ks[8], (L, LRU_WIDTH), jnp.float32, 0.9, 0.999)
    a0 = a_pow ** (1.0 / LRU_C)
    lru_lambda = jnp.log(a0) - jnp.log1p(-a0)
    lam_q1 = 0.1 * jax.random.normal(ks[9], (L, ATT_HEAD_DIM), jnp.float32)
    lam_k1 = 0.1 * jax.random.normal(ks[10], (L, ATT_HEAD_DIM), jnp.float32)
    lam_q2 = 0.1 * jax.random.normal(ks[11], (L, ATT_HEAD_DIM), jnp.float32)
    lam_k2 = 0.1 * jax.random.normal(ks[12], (L, ATT_HEAD_DIM), jnp.float32)
    subln_g = 1.0 + 0.01 * jax.random.normal(ks[13], (L, ATT_V_DIM), jnp.float32)
    w_pa = jax.random.normal(ks[14], (L, ATT_WIDTH, D), jnp.float32) * (ATT_WIDTH ** -0.5) * DN_BETA
    w_pb = jax.random.normal(ks[15], (L, LRU_WIDTH, D), jnp.float32) * (LRU_WIDTH ** -0.5) * DN_BETA
    w_out = jax.random.normal(ks[16], (L, D, D), jnp.float32) * (D ** -0.5) * DN_BETA
    ln_g = 1.0 + 0.01 * jax.random.normal(ks[17], (L, D), jnp.float32)
    ln_b = 0.01 * jax.random.normal(ks[18], (L, D), jnp.float32)
    return {"x": x, "w_in": w_in, "conv_w": conv_w, "conv_b": conv_b,
            "w_rgate": w_rgate, "b_rgate": b_rgate, "w_igate": w_igate, "b_igate": b_igate,
            "lru_lambda": lru_lambda, "lam_q1": lam_q1, "lam_k1": lam_k1,
            "lam_q2": lam_q2, "lam_k2": lam_k2, "subln_g": subln_g,
            "w_pa": w_pa, "w_pb": w_pb, "w_out": w_out, "ln_g": ln_g, "ln_b": ln_b}


def reference(x, w_in, conv_w, conv_b, w_rgate, b_rgate, w_igate, b_igate, lru_lambda,
              lam_q1, lam_k1, lam_q2, lam_k2, subln_g, w_pa, w_pb, w_out, ln_g, ln_b):
    B, S, D = x.shape
    for l in range(DEPTH):
        proj = jnp.einsum('bsd,dn->bsn', x, w_in[l])
        q, k, v, g_a, x_b, g_b, m_a, m_b = jnp.split(proj, SPLIT_IDX, axis=-1)
        q = q.reshape(B, S, ATT_HEADS, 2, ATT_HEAD_DIM)
        k = k.reshape(B, S, ATT_HEADS, 2, ATT_HEAD_DIM)
        v = v.reshape(B, S, ATT_HEADS, ATT_V_DIM)
        lam_init = 0.8 - 0.6 * math.exp(-0.3 * l)
        lam = (jnp.exp(jnp.sum(lam_q1[l].astype(jnp.float32) * lam_k1[l].astype(jnp.float32)))
               - jnp.exp(jnp.sum(lam_q2[l].astype(jnp.float32) * lam_k2[l].astype(jnp.float32)))
               + lam_init)
        att = _diff_attention(q, k, v, lam, lam_init, subln_g[l]) * jax.nn.silu(g_a)
        rec = _rg_lru(x_b, conv_w[l], conv_b[l], w_rgate[l], b_rgate[l],
                      w_igate[l], b_igate[l], lru_lambda[l]) * jax.nn.silu(g_b)
        merged = (jax.nn.sigmoid(m_a) * jnp.einsum('bse,ed->bsd', att, w_pa[l])
                  + jax.nn.sigmoid(m_b) * jnp.einsum('bse,ed->bsd', rec, w_pb[l]))
        out = jnp.einsum('bsd,de->bse', merged, w_out[l])
        x = _layer_norm(DN_ALPHA * x + out, ln_g[l], ln_b[l])
    return x
```

```python
import contextlib
import math
import numpy as np
import ml_dtypes
import concourse.bass as bass
import concourse.mybir as mybir
from concourse.bass_utils import run_bass_kernel_spmd

F32 = mybir.dt.float32
BF16 = mybir.dt.bfloat16
AF = mybir.ActivationFunctionType
ALU = mybir.AluOpType

ENGS = ("pe", "act", "dve", "pool", "sp")
EPOCH = 30000

S = 2048
D = 2048
L = 2
NH = 8
NOFF = 19
ALPHA = (2 * L) ** 0.25
C_OUT = 0.5 / ALPHA
LN_EPS_P = 1e-5 / (ALPHA * ALPHA)
SUBLN_EPS = 1e-5
LAM_INIT = [0.8 - 0.6 * math.exp(-0.3 * l) for l in range(L)]
SLOPES = [2.0 ** (-(h + 1)) for h in range(NH)]
HEAD_W = [128, 256, 512, 512, 512, 512, 512, 512]


class Res:
    __slots__ = ("name", "w", "r", "dsem")

    def __init__(self, name):
        self.name = name
        self.w = None
        self.r = {}
        self.dsem = None


class Prog:
    def __init__(self, nc):
        self.nc = nc
        self.ops = {e: [] for e in ENGS}
        self.cnt = {e: 0 for e in ENGS}
        self.epoch = {e: 0 for e in ENGS}
        self.seen = {e: {} for e in ENGS}
        self.semkeys = []
        self.dcnt = {}
        self.pending_unmarked = {e: False for e in ENGS}
        self.n_dsem = 0
        self.final = {}

    def _ekey(self, e):
        k = ("E", e, self.epoch[e])
        if k not in self.semkeys:
            self.semkeys.append(k)
        return k

    def _dkey(self, res, q):
        if res.dsem is None:
            res.dsem = {}
        kind = "sw" if q == "pool" else "hw"
        if kind not in res.dsem:
            k = ("D", self.n_dsem, kind)
            self.n_dsem += 1
            self.semkeys.append(k)
            self.dcnt[k] = 0
            res.dsem[kind] = k
        return res.dsem[kind]

    def _deps(self, reads, writes, extra):
        deps = []
        for r in reads:
            if r.w is not None:
                deps.append(r.w)
        for w in writes:
            if w.w is not None:
                deps.append(w.w)
            deps.extend(w.r.items())
        deps.extend(extra)
        return deps

    def _emit_waits(self, e, deps):
        for (k, v) in deps:
            if k[0] == "E" and k[1] == e and e == "pe":
                continue
            if self.seen[e].get(k, 0) >= v:
                continue
            self.seen[e][k] = v
            self.ops[e].append(("wait", k, v))

    def _update(self, tok, reads, writes):
        for r in reads:
            if r.r.get(tok[0], 0) < tok[1]:
                r.r[tok[0]] = tok[1]
        for w in writes:
            w.w = tok
            w.r = {}

    def op(self, e, fn, reads=(), writes=(), mark=True, extra=()):
        deps = self._deps(reads, writes, extra)
        self._emit_waits(e, deps)
        if mark:
            if self.cnt[e] >= EPOCH:
                self.final[(e, self.epoch[e])] = self.cnt[e]
                self.epoch[e] += 1
                self.cnt[e] = 0
            k = self._ekey(e)
            self.cnt[e] += 1
            tok = (k, self.cnt[e])
            self.ops[e].append(("op", fn, k))
            self.pending_unmarked[e] = False
        else:
            if self.cnt[e] >= EPOCH - 1:
                self.final[(e, self.epoch[e])] = self.cnt[e]
                self.epoch[e] += 1
                self.cnt[e] = 0
            k = self._ekey(e)
            tok = (k, self.cnt[e] + 1)
            self.ops[e].append(("op", fn, None))
            self.pending_unmarked[e] = True
        self._update(tok, reads, writes)
        return tok

    def dma(self, q, fn, reads=(), writes=(), semres=None, extra=()):
        deps = self._deps(reads, writes, extra)
        self._emit_waits(q, deps)
        k = self._dkey(semres, q)
        self.dcnt[k] += 16
        tok = (k, self.dcnt[k])
        self.ops[q].append(("dma", fn, k))
        self._update(tok, reads, writes)
        return tok

    def barrier(self):
        toks = []
        for e in ENGS:
            assert not self.pending_unmarked[e], e
            for ep in range(self.epoch[e] + 1):
                k = ("E", e, ep)
                if k in self.semkeys:
                    toks.append((k, self.cnt[e] if ep == self.epoch[e] else self.final[(e, ep)]))
        for k, v in self.dcnt.items():
            if v > 0:
                toks.append((k, v))
        toks = [t for t in toks if t[1] > 0]
        for e in ENGS:
            self._emit_waits(e, [t for t in toks if not (t[0][0] == "E" and t[0][1] == e)])

    def emit(self):
        nc = self.nc
        for e in ENGS:
            assert not self.pending_unmarked[e], e
        with contextlib.ExitStack() as st:
            sems = {}
            for i, k in enumerate(self.semkeys):
                sems[k] = st.enter_context(nc.semaphore("s%d" % i))
            block = st.enter_context(nc.Block())

            def run(e):
                def f(eng):
                    for it in self.ops[e]:
                        if it[0] == "wait":
                            eng.wait_ge(sems[it[1]], it[2])
                        elif it[0] == "op":
                            ins = it[1](eng)
                            if it[2] is not None:
                                ins.then_inc(sems[it[2]], 1)
                        else:
                            ins = it[1](eng)
                            ins.then_inc(sems[it[2]], 16)
                return f

            block.tensor(run("pe"))
            block.scalar(run("act"))
            block.vector(run("dve"))
            block.gpsimd(run("pool"))
            block.sync(run("sp"))


def I(name, *a, **k):
    def f(e):
        return getattr(e, name)(*a, **k)
    return f


def build_program(debug=False, stop_after=None):
    nc = bass.Bass("TRN2", target_bir_lowering=False)
    dbg = {}
    r_dbg = Res("dbg")

    def dump(name, src_ap, shape, dt, reads):
        if not debug:
            return
        d = nc.dram_tensor("dbg_" + name, list(shape), dt, kind="ExternalOutput").ap()
        P.dma("sp", I("dma_start", out=d, in_=src_ap), reads=reads, semres=r_dbg)
        P.barrier()

    def finish():
        P.barrier()
        P.emit()

    def din(name, shape, dt=F32):
        return nc.dram_tensor(name, list(shape), dt, kind="ExternalInput").ap()

    x_d = din("x", [S, D])
    w1_d = din("w1", [L * 80, 128, 2048])
    wp_d = din("wp", [L * 2 * 16, 128, 1024])
    wo_d = din("wo", [L * 16, 128, 2048])
    wg_d = din("wg", [L * 2, 128, 1024])
    vecs_d = din("vecs", [128, L * 8 * 8])
    sg_d = din("sg", [128, L])
    lnp_d = din("lnp", [L * 2, 128, D])
    lamv_d = din("lamv", [128, L * 4 * 64])
    ident_d = din("ident", [128, 128], BF16)
    tri_d = din("tri", [128, 256], BF16)
    abias_d = din("abias", [128, NH * NOFF])
    out_d = nc.dram_tensor("out", [S, D], F32, kind="ExternalOutput").ap()
    if debug:
        x1_d = nc.dram_tensor("x1_scratch", [S, D], F32, kind="ExternalOutput").ap()
        mrg_d = nc.dram_tensor("mrg_scratch", [16, 128, 16, 128], BF16, kind="ExternalOutput").ap()
    else:
        x1_d = nc.dram_tensor("x1_scratch", [S, D], F32).ap()
        mrg_d = nc.dram_tensor("mrg_scratch", [16, 128, 16, 128], BF16).ap()
    r_x1 = Res("x1_d")
    r_mrg = Res("mrg_d")

    P = Prog(nc)
    with contextlib.ExitStack() as st:
        def sb(name, shape, dt):
            return st.enter_context(nc.sbuf_tensor("sb_" + name, list(shape), dt))

        REG = [sb("regA", [128, 16 * 2048], BF16), sb("regB", [128, 16 * 2048], BF16)]
        r_reg = [[Res("reg%d_%d" % (g, i)) for i in range(16)] for g in range(2)]

        def row(g, i, c0=0, c1=2048):
            return REG[g][:, i * 2048 + c0: i * 2048 + c1]

        NSLOT = 6
        LOOKAHEAD = 3
        ring = [sb("ring%d" % i, [128, 2048], BF16) for i in range(NSLOT)]
        r_ring = [Res("ring%d" % i) for i in range(NSLOT)]

        ident = sb("ident", [128, 128], BF16); r_const = Res("const")
        tri = sb("tri", [128, 256], BF16)
        abias = sb("abias", [128, NH * NOFF], F32)
        ones_bf = sb("ones_bf", [128, 512], BF16)
        zeros_bf = sb("zeros_bf", [128, 128], BF16)
        nrm = sb("nrm", [128, 16], F32)
        ones_f = sb("ones_f", [128, 128], F32)
        vecs = sb("vecs", [128, L * 64], F32)
        sgt = sb("sgt", [128, L], F32)
        drv = sb("drv", [128, L * 40], F32)
        sc = sb("sc", [128, 32], F32)
        wgt = sb("wgt", [128, 2048], BF16); r_wgt = Res("wgt")
        tiny = sb("tiny", [128, 64], F32); r_tiny = Res("tiny")

        ARENA_BYTES = 46 * 1024
        arena = sb("arena", [128, ARENA_BYTES // 2], BF16)

        lt = sb("lt", [128, 64], F32)

        def av(off, n, dt):
            sz = 4 if dt == F32 else 2
            assert off % 4 == 0 and off + n * sz <= ARENA_BYTES, (off, n)
            v = arena[:, off // 2: off // 2 + n * sz // 2]
            return v.bitcast(F32) if dt == F32 else v

        pm = [st.enter_context(nc.psum_tensor("pm%d" % i, [128, 1024], F32)) for i in range(4)]
        r_bank = [Res("bank%d" % i) for i in range(8)]

        def bank(i, c0=0, c1=512):
            return pm[i // 2][:, (i % 2) * 512 + c0: (i % 2) * 512 + c1]

        def bank_bf(i):
            return pm[i // 2][:, (i % 2) * 512:(i % 2) * 512 + 512].bitcast(BF16)

        jobs = []
        state = {"issued": 0, "gate_open": set()}

        def add_job(dmas, gate=None):
            jobs.append(dict(dmas=dmas, gate=gate))
            return len(jobs) - 1

        def issue_upto(j_hi):
            while state["issued"] <= min(j_hi, len(jobs) - 1):
                j = state["issued"]
                job = jobs[j]
                if job["gate"] is not None and job["gate"] not in state["gate_open"]:
                    return
                s = j % NSLOT
                for (q, src, c0, c1, cast, rd) in job["dmas"]:
                    kw = dict(max_dma_last_dim=8192) if cast else {}
                    P.dma(q, I("dma_start", out=ring[s][:, c0:c1], in_=src, **kw),
                          reads=rd, writes=[r_ring[s]], semres=r_ring[s])
                state["issued"] += 1

        def get_w(j):
            issue_upto(j + LOOKAHEAD)
            assert state["issued"] > j, ("job not issued", j)
            s = j % NSLOT
            return ring[s], r_ring[s]

        J = {}
        for l in range(L):
            for h in range(NH):
                for kind, base in (("q", 0), ("k", 8), ("v", 16), ("ga", 24)):
                    J[(l, kind, h)] = add_job([("pool", w1_d[l * 80 + base + h], 0, 2048, True, [])])
            for n in range(8):
                J[(l, "xb", n)] = add_job([("pool", w1_d[l * 80 + 32 + n], 0, 2048, True, [])])
                J[(l, "gb", n)] = add_job([("pool", w1_d[l * 80 + 40 + n], 0, 2048, True, [])])
            for dc in range(16):
                J[(l, "ma", dc)] = add_job([("pool", w1_d[l * 80 + 48 + dc], 0, 2048, True, [])])
                J[(l, "mb", dc)] = add_job([("pool", w1_d[l * 80 + 64 + dc], 0, 2048, True, [])])
                J[(l, "pab", dc)] = add_job([("pool", wp_d[(l * 2 + 0) * 16 + dc], 0, 1024, True, []),
                                             ("pool", wp_d[(l * 2 + 1) * 16 + dc], 1024, 2048, True, [])])
            for tt in range(16):
                J[(l, "mg", tt)] = add_job([("sp", mrg_d[tt].rearrange("p d j -> p (d j)"), 0, 2048, False, [r_mrg])],
                                           gate=("p4", l))

        def cload(dst, src):
            P.dma("sp", I("dma_start", out=dst, in_=src), writes=[r_const], semres=r_const)

        cload(ident[:], ident_d)
        cload(tri[:], tri_d)
        cload(abias[:], abias_d)
        cload(vecs[:], vecs_d)
        cload(sgt[:], sg_d)
        lamv = av(32768, L * 256, F32)
        cload(lamv, lamv_d)
        P.op("dve", I("memset", tiny[:, 62:63], 1.0), writes=[r_tiny])
        P.op("dve", I("memset", ones_bf[:], 1.0), writes=[r_const])
        P.op("dve", I("memset", zeros_bf[:], 0.0), writes=[r_const])
        P.op("dve", I("memset", sc[:, 20:24], -0.5), writes=[r_const])
        P.op("dve", I("memset", ones_f[:], 1.0), writes=[r_const])
        P.op("dve", I("memset", sc[:, 16:17], SUBLN_EPS), writes=[r_const])
        P.op("dve", I("memset", sc[:, 17:18], LN_EPS_P), writes=[r_const])
        for l in range(L):
            vb = l * 64
            db = l * 40
            lam_ap = vecs[:, vb + 56: vb + 64]
            P.op("act", I("activation", out=tiny[:, 0:8], in_=lam_ap, func=AF.Exp, scale=-1.0),
                 reads=[r_const], writes=[r_tiny])
            P.op("act", I("activation", out=tiny[:, 8:16], in_=tiny[:, 0:8], func=AF.Ln, bias=tiny[:, 62:63], scale=1.0),
                 reads=[r_tiny], writes=[r_tiny])
            P.op("dve", I("tensor_scalar", out=drv[:, db: db + 8], in0=tiny[:, 8:16], scalar1=-4.0, scalar2=None, op0=ALU.mult),
                 reads=[r_tiny], writes=[r_const])
            P.op("dve", I("tensor_scalar", out=drv[:, db + 8: db + 16], in0=tiny[:, 8:16], scalar1=4.0, scalar2=None, op0=ALU.mult),
                 reads=[r_tiny], writes=[r_const])
            P.op("dve", I("tensor_scalar", out=drv[:, db + 16: db + 24], in0=vecs[:, vb + 40: vb + 48], scalar1=0.5, scalar2=None, op0=ALU.mult),
                 reads=[r_const], writes=[r_const])
            P.op("dve", I("tensor_scalar", out=drv[:, db + 24: db + 32], in0=vecs[:, vb + 48: vb + 56], scalar1=0.5, scalar2=None, op0=ALU.mult),
                 reads=[r_const], writes=[r_const])
            lb = l * 256
            P.op("dve", I("tensor_tensor", out=lt[:], in0=lamv[:, lb: lb + 64], in1=lamv[:, lb + 128: lb + 192], op=ALU.mult),
                 reads=[r_const, r_tiny], writes=[r_tiny])
            P.op("dve", I("reduce_sum", out=tiny[:, 50:51], in_=lt[:], axis=mybir.AxisListType.X), reads=[r_tiny], writes=[r_tiny])
            P.op("dve", I("tensor_tensor", out=lt[:], in0=lamv[:, lb + 64: lb + 128], in1=lamv[:, lb + 192: lb + 256], op=ALU.mult),
                 reads=[r_const, r_tiny], writes=[r_tiny])
            P.op("dve", I("reduce_sum", out=tiny[:, 51:52], in_=lt[:], axis=mybir.AxisListType.X), reads=[r_tiny], writes=[r_tiny])
            P.op("act", I("activation", out=tiny[:, 52:54], in_=tiny[:, 50:52], func=AF.Exp), reads=[r_tiny], writes=[r_tiny])
            P.op("dve", I("scalar_tensor_tensor", out=sc[:, 4 * l: 4 * l + 1], in0=tiny[:, 53:54], scalar=-LAM_INIT[l], in1=tiny[:, 52:53], op0=ALU.add, op1=ALU.subtract),
                 reads=[r_tiny], writes=[r_const])
            P.op("dve", I("tensor_scalar", out=sc[:, 4 * l + 2: 4 * l + 3], in0=sgt[:, l: l + 1], scalar1=0.5 * (1.0 - LAM_INIT[l]), scalar2=None, op0=ALU.mult),
                 reads=[r_const], writes=[r_const])

        def proj_fm(wslot, r_w, g, tb, bk, nk=16, src_row0=0):
            for kc in range(nk):
                P.op("pe", I("matmul", bank(bk), lhsT=wslot[:, kc * 128:(kc + 1) * 128],
                                                     rhs=row(g, src_row0 + kc, tb * 512, tb * 512 + 512),
                                                     start=(kc == 0), stop=(kc == nk - 1)),
                     reads=[r_w, r_reg[g][src_row0 + kc]], writes=[r_bank[bk]], mark=(kc == nk - 1))

        def make_xT(g, tt, xn_ap, r_xn, bk0=6):
            for half in range(2):
                bk = bk0 + half
                for i in range(8):
                    kc = half * 8 + i
                    P.op("pe", I("transpose", out=bank_bf(bk)[:, i * 128:(i + 1) * 128], in_=xn_ap[:, kc * 128:(kc + 1) * 128], identity=ident[:]),
                         reads=[r_xn, r_const], writes=[r_bank[bk]], mark=(i == 7))
                dst = REG[g][:].rearrange("p (k t) -> p k t", k=16)[:, half * 8: half * 8 + 8, tt * 128:(tt + 1) * 128]
                src = bank_bf(bk).rearrange("p (k t) -> p k t", k=8)
                P.op("dve", I("tensor_copy", out=dst, in_=src),
                     reads=[r_bank[bk]], writes=[r_reg[g][half * 8 + i] for i in range(8)])

        XY = [av(0, 2048, F32), av(8192, 2048, F32), av(36864, 2048, F32)]
        XN = av(16384, 2048, BF16)
        LNG = av(20480, 2048, F32)
        LNB = av(28672, 2048, F32)

        r_xy = [Res("xy0"), Res("xy1")]
        r_xn = Res("xn")
        for tt in range(16):
            b = tt % 2
            P.dma("sp", I("dma_start", out=XY[b], in_=x_d[tt * 128:(tt + 1) * 128, :]), writes=[r_xy[b]], semres=r_xy[b])
            P.op("act", I("activation", out=XN, in_=XY[b], func=AF.Copy), reads=[r_xy[b]], writes=[r_xn])
            make_xT(0, tt, XN, r_xn)
        issue_upto(LOOKAHEAD - 1)
        P.barrier()
        dump("xT0", REG[0][:], [128, 16 * 2048], BF16, r_reg[0])
        if stop_after == "pre":
            finish()
            return nc

        for l in range(L):
            gx = l % 2
            ga_ = 1 - gx
            vb = l * 64
            db = l * 40
            nlam = sc[:, 4 * l: 4 * l + 1]
            cgain = sc[:, 4 * l + 2: 4 * l + 3]
            for a in range(2):
                P.dma("pool", I("dma_start", out=wgt[:, a * 1024:(a + 1) * 1024], in_=wg_d[l * 2 + a], max_dma_last_dim=4096),
                      writes=[r_wgt], semres=r_wgt)

            VW = 130
            BS = []
            for bi in range(2):
                o = bi * 16448
                vv = av(o + 8192, 16 * VW, BF16)
                BS.append(dict(QT=av(o, 2048, BF16), r_qt=Res("qt%d" % bi), KT=av(o + 4096, 2048, BF16), r_kt=Res("kt%d" % bi),
                               VV=vv.rearrange("p (t w) -> p t w", t=16), r_vv=Res("vv%d" % bi),
                               GAS=av(o + 12352, 2048, BF16), r_gas=Res("gas%d" % bi)))
                P.op("dve", I("memset", BS[bi]["VV"][:, :, 128:130], 1.0), writes=[BS[bi]["r_vv"]])
            PT = [av(32896 + i * 1024, 512, BF16) for i in range(4)]; r_pt = [Res("pt%d" % i) for i in range(4)]
            U3 = av(36992, 512, F32).rearrange("p (s e) -> p s e", s=4); r_u = Res("u")
            ATM3 = av(39040, 512, BF16).rearrange("p (s e) -> p s e", s=4); r_atm = Res("atm")
            JUNK = av(40064, 128, BF16); r_junk = Res("junk")
            TG = av(40320, 512, F32); r_tg = Res("tg")
            r_nrm = Res("nrm")
            OV = [pm[2 + c][:].rearrange("p (s w) -> p s w", s=4) for c in range(2)]
            pctr = {"proj": 0, "step": 0, "prologue": True}

            def proj_units(h):
                bs = BS[h % 2]
                units = []

                def u_fm(kind, tb, dst, r_dst, use_act):
                    def f():
                        w, r_w = get_w(J[(l, kind, h)])
                        bk = (3, 0, 1, 2)[pctr["proj"] % 4] if pctr["prologue"] else 3
                        pctr["proj"] += 1
                        proj_fm(w, r_w, gx, tb, bk)
                        if kind == "ga":
                            P.op("act", I("activation", out=TG, in_=bank(bk), func=AF.Tanh, scale=0.5), reads=[r_bank[bk]], writes=[r_tg])
                            P.op("dve", I("scalar_tensor_tensor", out=dst[:, tb * 512:(tb + 1) * 512], in0=TG, scalar=1.0, in1=bank(bk), op0=ALU.add, op1=ALU.mult),
                                 reads=[r_tg, r_bank[bk]], writes=[r_dst])
                        elif use_act:
                            P.op("act", I("activation", out=dst[:, tb * 512:(tb + 1) * 512], in_=bank(bk), func=AF.Copy), reads=[r_bank[bk]], writes=[r_dst])
                        else:
                            P.op("dve", I("tensor_copy", out=dst[:, tb * 512:(tb + 1) * 512], in_=bank(bk)), reads=[r_bank[bk]], writes=[r_dst])
                    return f

                def u_v(g4):
                    def f():
                        w, r_w = get_w(J[(l, "v", h)])
                        bk = (3, 0, 1, 2)[pctr["proj"] % 4] if pctr["prologue"] else 3
                        pctr["proj"] += 1
                        for t4 in range(4):
                            tt = g4 * 4 + t4
                            c0 = t4 * 128
                            for kc in range(16):
                                P.op("pe", I("matmul", bank(bk, c0, c0 + 128), lhsT=row(gx, kc, tt * 128, tt * 128 + 128),
                                             rhs=w[:, kc * 128:(kc + 1) * 128], start=(kc == 0), stop=(kc == 15)),
                                     reads=[r_w, r_reg[gx][kc]], writes=[r_bank[bk]], mark=(kc == 15))
                        P.op("dve", I("tensor_copy", out=bs["VV"][:, g4 * 4:(g4 + 1) * 4, 0:128], in_=bank(bk).rearrange("p (t e) -> p t e", t=4)),
                             reads=[r_bank[bk]], writes=[bs["r_vv"]])
                    return f

                for tb in range(4):
                    units.append(u_fm("q", tb, bs["QT"], bs["r_qt"], False))
                for tb in range(4):
                    units.append(u_fm("k", tb, bs["KT"], bs["r_kt"], True))
                for g4 in range(4):
                    units.append(u_v(g4))
                for tb in range(4):
                    units.append(u_fm("ga", tb, bs["GAS"], bs["r_gas"], False))
                return units

            def group_steps(h, G0):
                W = HEAD_W[h]
                slope = SLOPES[h]
                steps = []
                for qs in range(G0, G0 + 512, W):
                    kt_hi = (qs + W) // 128 - 1
                    kt_lo = 0
                    while kt_lo < kt_hi and slope * (qs - (kt_lo * 128 + 127)) >= 128.0:
                        kt_lo += 1
                    for kt in range(kt_lo, kt_hi + 1):
                        for c in range(2):
                            steps.append(dict(qs=qs, W=W, kt=kt, c0=max(0, kt * 128 - qs), diag=(kt * 128 >= qs), c=c, sub0=(qs - G0) // 128))
                return steps

            def normalise(h, G0):
                bs = BS[h % 2]
                rb = [r_bank[4], r_bank[5], r_bank[6], r_bank[7]]
                P.op("dve", I("reciprocal", out=nrm[:, 0:4], in_=OV[0][:, :, 128]), reads=rb[0:2], writes=[r_nrm])
                P.op("dve", I("reciprocal", out=nrm[:, 4:8], in_=OV[1][:, :, 128]), reads=rb[2:4], writes=[r_nrm])
                P.op("dve", I("tensor_scalar", out=nrm[:, 4:8], in0=nrm[:, 4:8], scalar1=nlam, scalar2=None, op0=ALU.mult), reads=[r_nrm, r_const], writes=[r_nrm])
                for sub in range(4):
                    P.op("dve", I("tensor_scalar", out=U3[:, sub, :], in0=OV[0][:, sub, 0:128], scalar1=nrm[:, sub:sub + 1], scalar2=None, op0=ALU.mult),
                         reads=rb[0:2] + [r_nrm], writes=[r_u])
                    P.op("dve", I("scalar_tensor_tensor", out=U3[:, sub, :], in0=OV[1][:, sub, 0:128], scalar=nrm[:, 4 + sub:5 + sub], in1=U3[:, sub, :], op0=ALU.mult, op1=ALU.add),
                         reads=rb[2:4] + [r_nrm, r_u], writes=[r_u])
                for sub in range(4):
                    P.op("act", I("activation", out=JUNK, in_=U3[:, sub, :], func=AF.Square, accum_out=nrm[:, 8 + sub:9 + sub]),
                         reads=[r_u], writes=[r_junk, r_nrm])
                P.op("dve", I("tensor_scalar", out=nrm[:, 12:16], in0=nrm[:, 8:12], scalar1=1.0 / 128.0, scalar2=SUBLN_EPS, op0=ALU.mult, op1=ALU.add),
                     reads=[r_nrm], writes=[r_nrm])
                P.op("pool", I("tensor_tensor", out=nrm[:, 12:16], in0=nrm[:, 12:16], in1=sc[:, 20:24], op=ALU.pow), reads=[r_nrm, r_const], writes=[r_nrm])
                for sub in range(4):
                    P.op("dve", I("tensor_scalar", out=ATM3[:, sub, :], in0=U3[:, sub, :], scalar1=nrm[:, 12 + sub:13 + sub], scalar2=None, op0=ALU.mult),
                         reads=[r_u, r_nrm], writes=[r_atm])
                for sub in range(4):
                    P.op("pe", I("transpose", out=bank_bf(3)[:, sub * 128:(sub + 1) * 128], in_=ATM3[:, sub, :], identity=ident[:]),
                         reads=[r_atm, r_const], writes=[r_bank[3]], mark=(sub == 3))
                P.op("dve", I("scalar_tensor_tensor", out=row(ga_, h, G0, G0 + 512), in0=bank_bf(3)[:, 0:512], scalar=cgain, in1=bs["GAS"][:, G0:G0 + 512],
                              op0=ALU.mult, op1=ALU.mult),
                     reads=[r_bank[3], bs["r_gas"], r_const], writes=[r_reg[ga_][h]])

            def attention(h, units):
                bs = BS[h % 2]
                nu = len(units)
                ust = {"done": 0}
                groups = [group_steps(h, G0) for G0 in range(0, S, 512)]
                n_total = sum(len(g) for g in groups)
                flat = [(gi, st_) for gi, g in enumerate(groups) for st_ in g]

                def emit_qk(idx):
                    stp = flat[idx][1]
                    sb_ = pctr["step"] % 3
                    stp["sbank"] = sb_
                    stp["pt"] = pctr["step"] % 4
                    pctr["step"] += 1
                    c, kt, c0, qs, W = stp["c"], stp["kt"], stp["c0"], stp["qs"], stp["W"]
                    P.op("pe", I("matmul", bank(sb_, c0, W), lhsT=bs["KT"][c * 64:(c + 1) * 64, kt * 128:(kt + 1) * 128],
                                 rhs=bs["QT"][c * 64:(c + 1) * 64, qs + c0: qs + W], start=True, stop=True),
                         reads=[bs["r_kt"], bs["r_qt"]], writes=[r_bank[sb_]])

                def emit_units(upto):
                    while ust["done"] < min(upto, nu):
                        units[ust["done"]]()
                        ust["done"] += 1

                emit_qk(0)
                if n_total > 1:
                    emit_qk(1)
                opened = -1
                for i in range(n_total):
                    gi, stp = flat[i]
                    c, kt, c0, qs, W = stp["c"], stp["kt"], stp["c0"], stp["qs"], stp["W"]
                    sb_, pi = stp["sbank"], stp["pt"]
                    oi = (kt * 128 - qs) // 128 + 15
                    bias_ap = abias[:, h * NOFF + oi: h * NOFF + oi + 1]
                    P.op("act", I("activation", out=PT[pi][:, c0:W], in_=bank(sb_, c0, W), func=AF.Exp, scale=0.125, bias=bias_ap),
                         reads=[r_bank[sb_], r_const], writes=[r_pt[pi]])
                    if stp["diag"]:
                        P.op("dve", I("tensor_tensor", out=PT[pi][:, c0:c0 + 128], in0=PT[pi][:, c0:c0 + 128], in1=tri[:, 0:128], op=ALU.mult),
                             reads=[r_pt[pi], r_const], writes=[r_pt[pi]])
                    if i + 2 < n_total:
                        emit_qk(i + 2)
                    if opened != gi:
                        for b_ in range(4, 8):
                            P.op("pe", I("matmul", bank(b_), lhsT=zeros_bf[:], rhs=ones_bf[:], start=True, stop=True, skip_group_check=True),
                                 reads=[r_const], writes=[r_bank[b_]], mark=(b_ == 7))
                        opened = gi
                    subs = list(range(c0 // 128, W // 128))
                    for j in subs:
                        gsub = stp["sub0"] + j
                        P.op("pe", I("matmul", OV[c][:, gsub, 0:129], lhsT=PT[pi][:, j * 128:(j + 1) * 128], rhs=bs["VV"][:, kt, 0:129],
                                     start=False, stop=False, skip_group_check=True),
                             reads=[bs["r_vv"], r_pt[pi]], writes=[r_bank[4 + 2 * c + gsub // 2]], mark=(j == subs[-1]))
                    last_of_group = (i + 1 == n_total) or (flat[i + 1][0] != gi)
                    if last_of_group:
                        normalise(h, gi * 512)
                        emit_units(ust["done"] + 1)
                    emit_units(((i + 1) * nu) // n_total)
                emit_units(nu)

            for u in proj_units(0):
                u()
            pctr["prologue"] = False
            for h in range(NH):
                attention(h, proj_units(h + 1) if h + 1 < NH else [])
            issue_upto(J[(l, "xb", 0)] + LOOKAHEAD - 1)
            P.barrier()
            dump("att%d" % l, REG[ga_][:, 0:8 * 2048], [128, 8 * 2048], BF16, r_reg[ga_])
            if stop_after == "p1_%d" % l:
                finish()
                return nc

            XB = av(0, 2052, F32); r_xb = Res("xb")
            XC = av(8208, 2048, F32); r_xc = Res("xc")
            XCB = av(16400, 2048, BF16); r_xcb = Res("xcb")
            GBS = av(20496, 2048, BF16); r_gbs = Res("gbs")
            SQ = av(24592, 1024, F32); r_sq = Res("sq")
            T2 = av(28688, 1024, F32); r_t2 = Res("t2")
            HH = av(32784, 1024, F32); r_hh = Res("hh")
            TT = av(36880, 512, F32); r_ttt = Res("ttt")
            P.op("dve", I("memset", XB[:, 0:4], 0.0), writes=[r_xb])
            for n in range(8):
                wx, r_wx = get_w(J[(l, "xb", n)])
                for tb in range(4):
                    proj_fm(wx, r_wx, gx, tb, tb)
                    P.op("dve", I("tensor_copy", out=XB[:, 4 + tb * 512: 4 + (tb + 1) * 512], in_=bank(tb)), reads=[r_bank[tb]], writes=[r_xb])
                wgb, r_wgb = get_w(J[(l, "gb", n)])
                for tb in range(4):
                    proj_fm(wgb, r_wgb, gx, tb, 4 + tb)
                    P.op("act", I("activation", out=TT, in_=bank(4 + tb), func=AF.Tanh, scale=0.5), reads=[r_bank[4 + tb]], writes=[r_ttt])
                    P.op("dve", I("scalar_tensor_tensor", out=GBS[:, tb * 512:(tb + 1) * 512], in0=TT, scalar=1.0, in1=bank(4 + tb), op0=ALU.add, op1=ALU.mult),
                         reads=[r_ttt, r_bank[4 + tb]], writes=[r_gbs])
                cw = [vecs[:, vb + j * 8 + n: vb + j * 8 + n + 1] for j in range(4)]
                cb = vecs[:, vb + 32 + n: vb + 33 + n]
                P.op("dve", I("tensor_scalar", out=XC, in0=XB[:, 1:2049], scalar1=cw[0], scalar2=cb, op0=ALU.mult, op1=ALU.add),
                     reads=[r_xb, r_const], writes=[r_xc])
                for j in range(1, 4):
                    P.op("dve", I("scalar_tensor_tensor", out=XC, in0=XB[:, 1 + j: 2049 + j], scalar=cw[j], in1=XC, op0=ALU.mult, op1=ALU.add),
                         reads=[r_xb, r_xc, r_const], writes=[r_xc])
                P.op("act", I("activation", out=XCB, in_=XC, func=AF.Copy), reads=[r_xc], writes=[r_xcb])
                m4sp = drv[:, db + n: db + n + 1]
                p4sp = drv[:, db + 8 + n: db + 9 + n]
                hbr = drv[:, db + 16 + n: db + 17 + n]
                hbi = drv[:, db + 24 + n: db + 25 + n]
                T1 = XB[:, 4:1028]
                AA = XB[:, 1028:2052]
                for hf in range(2):
                    tk0 = hf * 1024
                    for gi in range(2):
                        for sub in range(2):
                            bk = gi * 2 + sub
                            P.op("pe", I("matmul", bank(bk), lhsT=wgt[:, gi * 1024 + n * 128: gi * 1024 + (n + 1) * 128],
                                rhs=XCB[:, tk0 + sub * 512: tk0 + (sub + 1) * 512], start=True, stop=True),
                                reads=[r_wgt, r_xcb], writes=[r_bank[bk]])
                    P.op("act", I("activation", out=T1, in_=pm[0][:], func=AF.Tanh, scale=0.5, bias=hbr),
                         reads=[r_bank[0], r_bank[1], r_const, r_xc], writes=[r_xb])
                    P.op("act", I("activation", out=AA, in_=T1, func=AF.Exp, scale=m4sp, bias=m4sp), reads=[r_xb, r_const], writes=[r_xb])
                    P.op("act", I("activation", out=T1, in_=T1, func=AF.Tanh, scale=p4sp, bias=p4sp), reads=[r_xb, r_const], writes=[r_xb])
                    P.op("act", I("activation", out=T2, in_=pm[1][:], func=AF.Tanh, scale=0.5, bias=hbi),
                         reads=[r_bank[2], r_bank[3], r_const], writes=[r_t2])
                    P.op("dve", I("tensor_tensor", out=SQ, in0=AA, in1=AA, op=ALU.mult), reads=[r_xb], writes=[r_sq])
                    P.op("dve", I("scalar_tensor_tensor", out=SQ, in0=SQ, scalar=1.0, in1=T1, op0=ALU.add, op1=ALU.mult), reads=[r_sq, r_xb], writes=[r_sq])
                    P.op("act", I("activation", out=SQ, in_=SQ, func=AF.Sqrt), reads=[r_sq], writes=[r_sq])
                    P.op("dve", I("scalar_tensor_tensor", out=T2, in0=T2, scalar=1.0, in1=XC[:, tk0: tk0 + 1024], op0=ALU.add, op1=ALU.mult),
                         reads=[r_t2, r_xc], writes=[r_t2])
                    P.op("dve", I("scalar_tensor_tensor", out=T2, in0=T2, scalar=0.5, in1=SQ, op0=ALU.mult, op1=ALU.mult), reads=[r_t2, r_sq], writes=[r_t2])
                    if hf == 0:
                        P.op("dve", I("tensor_tensor_scan", out=HH, data0=AA, data1=T2, initial=0.0, op0=ALU.mult, op1=ALU.add),
                             reads=[r_xb, r_t2], writes=[r_hh])
                    else:
                        P.op("dve", I("tensor_copy", out=tiny[:, 60:61], in_=HH[:, 1023:1024]), reads=[r_hh], writes=[r_tiny])
                        P.op("dve", I("tensor_tensor_scan", out=HH, data0=AA, data1=T2, initial=tiny[:, 60:61], op0=ALU.mult, op1=ALU.add),
                             reads=[r_xb, r_t2, r_tiny], writes=[r_hh])
                    P.op("dve", I("scalar_tensor_tensor", out=row(ga_, 8 + n, tk0, tk0 + 1024), in0=HH, scalar=0.5, in1=GBS[:, tk0: tk0 + 1024], op0=ALU.mult, op1=ALU.mult),
                         reads=[r_hh, r_gbs], writes=[r_reg[ga_][8 + n]])
            issue_upto(J[(l, "ma", 0)] + LOOKAHEAD - 1)
            P.barrier()
            dump("rec%d" % l, REG[ga_][:, 8 * 2048:16 * 2048], [128, 8 * 2048], BF16, r_reg[ga_])
            if stop_after == "p2_%d" % l:
                finish()
                return nc

            TA = [av(0, 512, F32), av(2048, 512, F32)]; r_ta = [Res("ta0"), Res("ta1")]
            TB = [av(4096, 512, F32), av(6144, 512, F32)]; r_tb = [Res("tb0"), Res("tb1")]
            MG = [av(8192, 2048, BF16), av(12288, 2048, BF16)]; r_mg = [Res("mg0"), Res("mg1")]
            for dc in range(16):
                wa, r_wa = get_w(J[(l, "ma", dc)])
                wb, r_wb = get_w(J[(l, "mb", dc)])
                wpab, r_wpab = get_w(J[(l, "pab", dc)])
                mb_ = dc % 2
                for tb in range(4):
                    s_ = tb % 2
                    b0 = 4 * s_
                    proj_fm(wa, r_wa, gx, tb, b0 + 0)
                    proj_fm(wb, r_wb, gx, tb, b0 + 1)
                    proj_fm(wpab[:, 0:1024], r_wpab, ga_, tb, b0 + 2, nk=8, src_row0=0)
                    proj_fm(wpab[:, 1024:2048], r_wpab, ga_, tb, b0 + 3, nk=8, src_row0=8)
                    P.op("act", I("activation", out=TA[s_], in_=bank(b0), func=AF.Tanh, scale=0.5), reads=[r_bank[b0]], writes=[r_ta[s_]])
                    P.op("act", I("activation", out=TB[s_], in_=bank(b0 + 1), func=AF.Tanh, scale=0.5), reads=[r_bank[b0 + 1]], writes=[r_tb[s_]])
                    P.op("dve", I("scalar_tensor_tensor", out=TA[s_], in0=TA[s_], scalar=1.0, in1=bank(b0 + 2), op0=ALU.add, op1=ALU.mult),
                         reads=[r_ta[s_], r_bank[b0 + 2]], writes=[r_ta[s_]])
                    P.op("dve", I("scalar_tensor_tensor", out=TB[s_], in0=TB[s_], scalar=1.0, in1=bank(b0 + 3), op0=ALU.add, op1=ALU.mult),
                         reads=[r_tb[s_], r_bank[b0 + 3]], writes=[r_tb[s_]])
                    P.op("dve", I("tensor_tensor", out=MG[mb_][:, tb * 512:(tb + 1) * 512], in0=TA[s_], in1=TB[s_], op=ALU.add),
                         reads=[r_ta[s_], r_tb[s_]], writes=[r_mg[mb_]])
                dst = mrg_d.rearrange("t p d j -> p t d j")[:, :, dc, :]
                src = MG[mb_].rearrange("p (t j) -> p t j", t=16)
                P.dma("sp", I("dma_start", out=dst, in_=src), reads=[r_mg[mb_]], writes=[r_mrg], semres=r_mg[mb_])
            P.barrier()
            if stop_after == "p3_%d" % l:
                finish()
                return nc

            state["gate_open"].add(("p4", l))
            gw = gx
            for dc in range(16):
                P.dma("pool", I("dma_start", out=row(gw, dc), in_=wo_d[l * 16 + dc], max_dma_last_dim=8192),
                      writes=[r_reg[gw][dc]], semres=r_reg[gw][dc])
            r_ln = Res("ln")
            P.dma("sp", I("dma_start", out=LNG, in_=lnp_d[l * 2 + 0]), writes=[r_ln], semres=r_ln)
            P.dma("sp", I("dma_start", out=LNB, in_=lnp_d[l * 2 + 1]), writes=[r_ln], semres=r_ln)
            r_xy = [Res("xy0"), Res("xy1"), Res("xy2")]
            r_xn = Res("xn")
            src_x = x_d if l == 0 else x1_d
            dst_x = x1_d if l == 0 else out_d
            pending_T = None
            for tt in range(16):
                b = tt % 3
                mgw, r_mgw = get_w(J[(l, "mg", tt)])
                P.dma("sp", I("dma_start", out=XY[b], in_=src_x[tt * 128:(tt + 1) * 128, :]),
                      reads=([r_x1] if l == 1 else []), writes=[r_xy[b]], semres=r_xy[b])
                bset = 4 * (tt % 2)
                for eb in range(4):
                    bk = bset + eb
                    for dc in range(16):
                        P.op("pe", I("matmul", bank(bk), lhsT=mgw[:, dc * 128:(dc + 1) * 128],
                                                                                 rhs=row(gw, dc, eb * 512, eb * 512 + 512), start=(dc == 0), stop=(dc == 15)),
                             reads=[r_mgw, r_reg[gw][dc]], writes=[r_bank[bk]], mark=(dc == 15))
                    P.op("dve", I("scalar_tensor_tensor", out=XY[b][:, eb * 512:(eb + 1) * 512], in0=bank(bk), scalar=C_OUT,
                                                                                 in1=XY[b][:, eb * 512:(eb + 1) * 512], op0=ALU.mult, op1=ALU.add),
                         reads=[r_bank[bk], r_xy[b]], writes=[r_xy[b]])
                    P.op("dve", I("bn_stats", out=tiny[:, eb * 6:(eb + 1) * 6], in_=XY[b][:, eb * 512:(eb + 1) * 512]), reads=[r_xy[b]], writes=[r_tiny])
                    if eb == 3 and pending_T is not None:
                        make_xT(ga_, pending_T[0], XN, r_xn, bk0=pending_T[1])
                        pending_T = None
                P.op("dve", I("bn_aggr", out=tiny[:, 24:26], in_=tiny[:, 0:24]), reads=[r_tiny], writes=[r_tiny])
                P.op("act", I("activation", out=tiny[:, 26:27], in_=tiny[:, 25:26], func=AF.Sqrt, scale=1.0, bias=sc[:, 17:18]), reads=[r_tiny, r_const], writes=[r_tiny])
                P.op("dve", I("reciprocal", out=tiny[:, 27:28], in_=tiny[:, 26:27]), reads=[r_tiny], writes=[r_tiny])
                P.op("dve", I("scalar_tensor_tensor", out=tiny[:, 28:29], in0=tiny[:, 24:25], scalar=-1.0, in1=tiny[:, 27:28], op0=ALU.mult, op1=ALU.mult),
                     reads=[r_tiny], writes=[r_tiny])
                P.op("act", I("activation", out=XY[b], in_=XY[b], func=AF.Identity, scale=tiny[:, 27:28], bias=tiny[:, 28:29]),
                     reads=[r_xy[b], r_tiny], writes=[r_xy[b]])
                P.op("dve", I("tensor_tensor", out=XY[b], in0=XY[b], in1=LNG, op=ALU.mult), reads=[r_xy[b], r_ln], writes=[r_xy[b]])
                P.op("pool", I("tensor_tensor", out=XY[b], in0=XY[b], in1=LNB, op=ALU.add), reads=[r_xy[b], r_ln], writes=[r_xy[b]])
                P.dma("sp", I("dma_start", out=dst_x[tt * 128:(tt + 1) * 128, :], in_=XY[b]),
                      reads=[r_xy[b]], writes=([r_x1] if l == 0 else []), semres=r_xy[b])
                if l < L - 1:
                    P.op("act", I("activation", out=XN, in_=XY[b], func=AF.Copy), reads=[r_xy[b]], writes=[r_xn])
                    pending_T = (tt, bset)
            if pending_T is not None:
                make_xT(ga_, pending_T[0], XN, r_xn, bk0=pending_T[1])
                pending_T = None
            if l < L - 1:
                issue_upto(J[(l + 1, "q", 0)] + LOOKAHEAD - 1)
            P.barrier()
            if stop_after == "p4_%d" % l:
                finish()
                return nc

        P.emit()
    return nc


_NC_CACHE = {}


def _prep_shared(w_in, conv_w, conv_b, w_rgate, b_rgate, w_igate, b_igate, lru_lambda,
                 lam_q1, lam_k1, lam_q2, lam_k2, subln_g, w_pa, w_pb, w_out, ln_g, ln_b):
    f = np.float32
    w1 = np.ascontiguousarray(w_in.astype(f, copy=False).reshape(L, 16, 128, 80, 128).transpose(0, 3, 2, 1, 4)).reshape(L * 80, 128, 2048)
    wp = np.stack([w_pa, w_pb], axis=1).astype(f, copy=False)
    wp = np.ascontiguousarray(wp.reshape(L, 2, 8, 128, 16, 128).transpose(0, 1, 4, 3, 2, 5)).reshape(L * 2 * 16, 128, 1024)
    wo = np.ascontiguousarray(w_out.astype(f, copy=False).reshape(L * 16, 128, 2048))
    wg = np.stack([w_rgate, w_igate], axis=1).astype(f, copy=False)
    wg = np.ascontiguousarray(wg.transpose(0, 1, 3, 2, 4)).reshape(L * 2, 128, 1024)
    vs = np.concatenate([conv_w.astype(f, copy=False),
                         conv_b[:, None, :], b_rgate[:, None, :], b_igate[:, None, :], lru_lambda[:, None, :]], axis=1)
    vecs = np.ascontiguousarray(vs.reshape(L, 8, 8, 128).transpose(3, 0, 1, 2)).reshape(128, L * 64).astype(f)
    sg = np.ascontiguousarray(subln_g.astype(f, copy=False).T)
    lnp = np.stack([ln_g, ln_b], axis=1).astype(f, copy=False)
    lnp = np.ascontiguousarray(np.broadcast_to(lnp[:, :, None, :], (L, 2, 128, D))).reshape(L * 2, 128, D)
    lv = np.stack([lam_q1, lam_q2, lam_k1, lam_k2], axis=1).astype(f, copy=False).reshape(1, L * 256)
    lamv = np.ascontiguousarray(np.broadcast_to(lv, (128, L * 256)))
    ident = np.eye(128, dtype=np.float32).astype(ml_dtypes.bfloat16)
    ki = np.arange(128)
    tri1 = (ki[None, :] >= ki[:, None]).astype(np.float32)
    tri = np.concatenate([tri1, tri1], axis=1).astype(ml_dtypes.bfloat16)
    offs = np.arange(NOFF) - 15
    ab = np.zeros((128, NH, NOFF), np.float32)
    for h in range(NH):
        ab[:, h, :] = SLOPES[h] * (ki[:, None] + 128.0 * offs[None, :])
    abias = ab.reshape(128, NH * NOFF)
    return dict(w1=w1, wp=wp, wo=wo, wg=wg, vecs=vecs, sg=sg, lnp=lnp, lamv=lamv, ident=ident, tri=tri, abias=abias)


def kernel(x, w_in, conv_w, conv_b, w_rgate, b_rgate, w_igate, b_igate, lru_lambda,
           lam_q1, lam_k1, lam_q2, lam_k2, subln_g, w_pa, w_pb, w_out, ln_g, ln_b):
    x = np.asarray(x, dtype=np.float32)
    args = [np.asarray(a, dtype=np.float32) for a in (w_in, conv_w, conv_b, w_rgate, b_rgate, w_igate, b_igate, lru_lambda,
                                                      lam_q1, lam_k1, lam_q2, lam_k2, subln_g, w_pa, w_pb, w_out, ln_g, ln_b)]
    shared = _prep_shared(*args)
    if "nc" not in _NC_CACHE:
        _NC_CACHE["nc"] = build_program()
    nc = _NC_CACHE["nc"]
    n = x.shape[0]
    in_maps = []
    for b in range(n):
        m = dict(shared)
        m["x"] = np.ascontiguousarray(x[b])
        in_maps.append(m)
    res = run_bass_kernel_spmd(nc, in_maps, core_ids=list(range(n)))
    return np.stack([np.asarray(r["out"], dtype=np.float32) for r in res.results], axis=0)
```

```python
import contextlib
import math
import numpy as np
import ml_dtypes
import concourse.bass as bass
import concourse.mybir as mybir
from concourse.bass_utils import run_bass_kernel_spmd

F32 = mybir.dt.float32
BF16 = mybir.dt.bfloat16
AF = mybir.ActivationFunctionType
ALU = mybir.AluOpType

ENGS = ("pe", "act", "dve", "pool", "sp")
EPOCH = 30000

S = 2048
D = 2048
L = 2
NH = 8
NOFF = 19
ALPHA = (2 * L) ** 0.25
C_OUT = 0.5 / ALPHA
LN_EPS_P = 1e-5 / (ALPHA * ALPHA)
SUBLN_EPS = 1e-5
LAM_INIT = [0.8 - 0.6 * math.exp(-0.3 * l) for l in range(L)]
SLOPES = [2.0 ** (-(h + 1)) for h in range(NH)]
HEAD_W = [128, 256, 512, 512, 512, 512, 512, 512]


class Res:
    __slots__ = ("name", "w", "r", "dsem")

    def __init__(self, name):
        self.name = name
        self.w = None
        self.r = {}
        self.dsem = None


class Prog:
    def __init__(self, nc):
        self.nc = nc
        self.ops = {e: [] for e in ENGS}
        self.cnt = {e: 0 for e in ENGS}
        self.epoch = {e: 0 for e in ENGS}
        self.seen = {e: {} for e in ENGS}
        self.semkeys = []
        self.dcnt = {}
        self.pending_unmarked = {e: False for e in ENGS}
        self.n_dsem = 0
        self.final = {}

    def _ekey(self, e):
        k = ("E", e, self.epoch[e])
        if k not in self.semkeys:
            self.semkeys.append(k)
        return k

    def _dkey(self, res, q):
        if res.dsem is None:
            res.dsem = {}
        kind = "sw" if q == "pool" else "hw"
        if kind not in res.dsem:
            k = ("D", self.n_dsem, kind)
            self.n_dsem += 1
            self.semkeys.append(k)
            self.dcnt[k] = 0
            res.dsem[kind] = k
        return res.dsem[kind]

    def _deps(self, reads, writes, extra):
        deps = []
        for r in reads:
            if r.w is not None:
                deps.append(r.w)
        for w in writes:
            if w.w is not None:
                deps.append(w.w)
            deps.extend(w.r.items())
        deps.extend(extra)
        return deps

    def _emit_waits(self, e, deps):
        for (k, v) in deps:
            if k[0] == "E" and k[1] == e and e == "pe":
                continue
            if self.seen[e].get(k, 0) >= v:
                continue
            self.seen[e][k] = v
            self.ops[e].append(("wait", k, v))

    def _update(self, tok, reads, writes):
        for r in reads:
            if r.r.get(tok[0], 0) < tok[1]:
                r.r[tok[0]] = tok[1]
        for w in writes:
            w.w = tok
            w.r = {}

    def op(self, e, fn, reads=(), writes=(), mark=True, extra=()):
        deps = self._deps(reads, writes, extra)
        self._emit_waits(e, deps)
        if mark:
            if self.cnt[e] >= EPOCH:
                self.final[(e, self.epoch[e])] = self.cnt[e]
                self.epoch[e] += 1
                self.cnt[e] = 0
            k = self._ekey(e)
            self.cnt[e] += 1
            tok = (k, self.cnt[e])
            self.ops[e].append(("op", fn, k))
            self.pending_unmarked[e] = False
        else:
            if self.cnt[e] >= EPOCH - 1:
                self.final[(e, self.epoch[e])] = self.cnt[e]
                self.epoch[e] += 1
                self.cnt[e] = 0
            k = self._ekey(e)
            tok = (k, self.cnt[e] + 1)
            self.ops[e].append(("op", fn, None))
            self.pending_unmarked[e] = True
        self._update(tok, reads, writes)
        return tok

    def dma(self, q, fn, reads=(), writes=(), semres=None, extra=()):
        deps = self._deps(reads, writes, extra)
        self._emit_waits(q, deps)
        k = self._dkey(semres, q)
        self.dcnt[k] += 16
        tok = (k, self.dcnt[k])
        self.ops[q].append(("dma", fn, k))
        self._update(tok, reads, writes)
        return tok

    def barrier(self):
        toks = []
        for e in ENGS:
            assert not self.pending_unmarked[e], e
            for ep in range(self.epoch[e] + 1):
                k = ("E", e, ep)
                if k in self.semkeys:
                    toks.append((k, self.cnt[e] if ep == self.epoch[e] else self.final[(e, ep)]))
        for k, v in self.dcnt.items():
            if v > 0:
                toks.append((k, v))
        toks = [t for t in toks if t[1] > 0]
        for e in ENGS:
            self._emit_waits(e, [t for t in toks if not (t[0][0] == "E" and t[0][1] == e)])

    def emit(self):
        nc = self.nc
        for e in ENGS:
            assert not self.pending_unmarked[e], e
        with contextlib.ExitStack() as st:
            sems = {}
            for i, k in enumerate(self.semkeys):
                sems[k] = st.enter_context(nc.semaphore("s%d" % i))
            block = st.enter_context(nc.Block())

            def run(e):
                def f(eng):
                    for it in self.ops[e]:
                        if it[0] == "wait":
                            eng.wait_ge(sems[it[1]], it[2])
                        elif it[0] == "op":
                            ins = it[1](eng)
                            if it[2] is not None:
                                ins.then_inc(sems[it[2]], 1)
                        else:
                            ins = it[1](eng)
                            ins.then_inc(sems[it[2]], 16)
                return f

            block.tensor(run("pe"))
            block.scalar(run("act"))
            block.vector(run("dve"))
            block.gpsimd(run("pool"))
            block.sync(run("sp"))


def I(name, *a, **k):
    def f(e):
        return getattr(e, name)(*a, **k)
    return f


def build_program(debug=False, stop_after=None):
    nc = bass.Bass("TRN2", target_bir_lowering=False)
    dbg = {}
    r_dbg = Res("dbg")

    def dump(name, src_ap, shape, dt, reads):
        if not debug:
            return
        d = nc.dram_tensor("dbg_" + name, list(shape), dt, kind="ExternalOutput").ap()
        P.dma("sp", I("dma_start", out=d, in_=src_ap), reads=reads, semres=r_dbg)
        P.barrier()

    def finish():
        P.barrier()
        P.emit()

    def din(name, shape, dt=F32):
        return nc.dram_tensor(name, list(shape), dt, kind="ExternalInput").ap()

    x_d = din("x", [S, D])
    w1_d = din("w1", [L * 80, 128, 2048])
    wp_d = din("wp", [L * 2 * 16, 128, 1024])
    wo_d = din("wo", [L * 16, 128, 2048])
    wg_d = din("wg", [L * 2, 128, 1024])
    vecs_d = din("vecs", [128, L * 8 * 8])
    sg_d = din("sg", [128, L])
    lnp_d = din("lnp", [L * 2, 128, D])
    lamv_d = din("lamv", [128, L * 4 * 64])
    ident_d = din("ident", [128, 128], BF16)
    tri_d = din("tri", [128, 256], BF16)
    abias_d = din("abias", [128, NH * NOFF])
    out_d = nc.dram_tensor("out", [S, D], F32, kind="ExternalOutput").ap()
    if debug:
        x1_d = nc.dram_tensor("x1_scratch", [S, D], F32, kind="ExternalOutput").ap()
        mrg_d = nc.dram_tensor("mrg_scratch", [16, 128, 16, 128], BF16, kind="ExternalOutput").ap()
    else:
        x1_d = nc.dram_tensor("x1_scratch", [S, D], F32).ap()
        mrg_d = nc.dram_tensor("mrg_scratch", [16, 128, 16, 128], BF16).ap()
    r_x1 = Res("x1_d")
    r_mrg = Res("mrg_d")

    P = Prog(nc)
    with contextlib.ExitStack() as st:
        def sb(name, shape, dt):
            return st.enter_context(nc.sbuf_tensor("sb_" + name, list(shape), dt))

        REG = [sb("regA", [128, 16 * 2048], BF16), sb("regB", [128, 16 * 2048], BF16)]
        r_reg = [[Res("reg%d_%d" % (g, i)) for i in range(16)] for g in range(2)]

        def row(g, i, c0=0, c1=2048):
            return REG[g][:, i * 2048 + c0: i * 2048 + c1]

        NSLOT = 6
        LOOKAHEAD = 3
        ring = [sb("ring%d" % i, [128, 2048], BF16) for i in range(NSLOT)]
        r_ring = [Res("ring%d" % i) for i in range(NSLOT)]

        ident = sb("ident", [128, 128], BF16); r_const = Res("const")
        tri = sb("tri", [128, 256], BF16)
        abias = sb("abias", [128, NH * NOFF], F32)
        ones_bf = sb("ones_bf", [128, 512], BF16)
        zeros_bf = sb("zeros_bf", [128, 128], BF16)
        nrm = sb("nrm", [128, 16], F32)
        vecs = sb("vecs", [128, L * 64], F32)
        sgt = sb("sgt", [128, L], F32)
        drv = sb("drv", [128, L * 40], F32)
        sc = sb("sc", [128, 32], F32)
        wgt = sb("wgt", [128, 2048], BF16); r_wgt = Res("wgt")
        tiny = sb("tiny", [128, 64], F32); r_tiny = Res("tiny")

        ARENA_BYTES = 46 * 1024
        arena = sb("arena", [128, ARENA_BYTES // 2], BF16)

        lt = sb("lt", [128, 64], F32)

        def av(off, n, dt):
            sz = 4 if dt == F32 else 2
            assert off % 4 == 0 and off + n * sz <= ARENA_BYTES, (off, n)
            v = arena[:, off // 2: off // 2 + n * sz // 2]
            return v.bitcast(F32) if dt == F32 else v

        pm = [st.enter_context(nc.psum_tensor("pm%d" % i, [128, 1024], F32)) for i in range(4)]
        r_bank = [Res("bank%d" % i) for i in range(8)]

        def bank(i, c0=0, c1=512):
            return pm[i // 2][:, (i % 2) * 512 + c0: (i % 2) * 512 + c1]

        def bank_bf(i):
            return pm[i // 2][:, (i % 2) * 512:(i % 2) * 512 + 512].bitcast(BF16)

        jobs = []
        state = {"issued": 0, "gate_open": set()}

        def add_job(dmas, gate=None):
            jobs.append(dict(dmas=dmas, gate=gate))
            return len(jobs) - 1

        def issue_upto(j_hi):
            while state["issued"] <= min(j_hi, len(jobs) - 1):
                j = state["issued"]
                job = jobs[j]
                if job["gate"] is not None and job["gate"] not in state["gate_open"]:
                    return
                s = j % NSLOT
                for (q, src, c0, c1, cast, rd) in job["dmas"]:
                    kw = dict(max_dma_last_dim=8192) if cast else {}
                    P.dma(q, I("dma_start", out=ring[s][:, c0:c1], in_=src, **kw),
                          reads=rd, writes=[r_ring[s]], semres=r_ring[s])
                state["issued"] += 1

        def get_w(j):
            issue_upto(j + LOOKAHEAD)
            assert state["issued"] > j, ("job not issued", j)
            s = j % NSLOT
            return ring[s], r_ring[s]

        J = {}
        for l in range(L):
            for h in range(NH):
                for kind, base in (("q", 0), ("k", 8), ("v", 16), ("ga", 24)):
                    J[(l, kind, h)] = add_job([("pool", w1_d[l * 80 + base + h], 0, 2048, True, [])])
            for n in range(8):
                J[(l, "xb", n)] = add_job([("pool", w1_d[l * 80 + 32 + n], 0, 2048, True, [])])
                J[(l, "gb", n)] = add_job([("pool", w1_d[l * 80 + 40 + n], 0, 2048, True, [])])
            for dc in range(16):
                J[(l, "ma", dc)] = add_job([("pool", w1_d[l * 80 + 48 + dc], 0, 2048, True, [])])
                J[(l, "mb", dc)] = add_job([("pool", w1_d[l * 80 + 64 + dc], 0, 2048, True, [])])
                J[(l, "pab", dc)] = add_job([("pool", wp_d[(l * 2 + 0) * 16 + dc], 0, 1024, True, []),
                                             ("pool", wp_d[(l * 2 + 1) * 16 + dc], 1024, 2048, True, [])])
            for tt in range(16):
                J[(l, "mg", tt)] = add_job([("sp", mrg_d[tt].rearrange("p d j -> p (d j)"), 0, 2048, False, [r_mrg])],
                                           gate=("p4", l))

        def cload(dst, src):
            P.dma("sp", I("dma_start", out=dst, in_=src), writes=[r_const], semres=r_const)

        cload(ident[:], ident_d)
        cload(tri[:], tri_d)
        cload(abias[:], abias_d)
        cload(vecs[:], vecs_d)
        cload(sgt[:], sg_d)
        lamv = av(32768, L * 256, F32)
        cload(lamv, lamv_d)
        P.op("dve", I("memset", tiny[:, 62:63], 1.0), writes=[r_tiny])
        P.op("dve", I("memset", ones_bf[:], 1.0), writes=[r_const])
        P.op("dve", I("memset", zeros_bf[:], 0.0), writes=[r_const])
        P.op("dve", I("memset", sc[:, 20:24], -0.5), writes=[r_const])
        P.op("dve", I("memset", sc[:, 16:17], SUBLN_EPS), writes=[r_const])
        P.op("dve", I("memset", sc[:, 17:18], LN_EPS_P), writes=[r_const])
        for l in range(L):
            vb = l * 64
            db = l * 40
            lam_ap = vecs[:, vb + 56: vb + 64]
            P.op("act", I("activation", out=tiny[:, 0:8], in_=lam_ap, func=AF.Exp, scale=-1.0),
                 reads=[r_const], writes=[r_tiny])
            P.op("act", I("activation", out=tiny[:, 8:16], in_=tiny[:, 0:8], func=AF.Ln, bias=tiny[:, 62:63], scale=1.0),
                 reads=[r_tiny], writes=[r_tiny])
            P.op("dve", I("tensor_scalar", out=drv[:, db: db + 8], in0=tiny[:, 8:16], scalar1=-4.0, scalar2=None, op0=ALU.mult),
                 reads=[r_tiny], writes=[r_const])
            P.op("dve", I("tensor_scalar", out=drv[:, db + 8: db + 16], in0=tiny[:, 8:16], scalar1=4.0, scalar2=None, op0=ALU.mult),
                 reads=[r_tiny], writes=[r_const])
            P.op("dve", I("tensor_scalar", out=drv[:, db + 16: db + 24], in0=vecs[:, vb + 40: vb + 48], scalar1=0.5, scalar2=None, op0=ALU.mult),
                 reads=[r_const], writes=[r_const])
            P.op("dve", I("tensor_scalar", out=drv[:, db + 24: db + 32], in0=vecs[:, vb + 48: vb + 56], scalar1=0.5, scalar2=None, op0=ALU.mult),
                 reads=[r_const], writes=[r_const])
            lb = l * 256
            P.op("dve", I("tensor_tensor", out=lt[:], in0=lamv[:, lb: lb + 64], in1=lamv[:, lb + 128: lb + 192], op=ALU.mult),
                 reads=[r_const, r_tiny], writes=[r_tiny])
            P.op("dve", I("reduce_sum", out=tiny[:, 50:51], in_=lt[:], axis=mybir.AxisListType.X), reads=[r_tiny], writes=[r_tiny])
            P.op("dve", I("tensor_tensor", out=lt[:], in0=lamv[:, lb + 64: lb + 128], in1=lamv[:, lb + 192: lb + 256], op=ALU.mult),
                 reads=[r_const, r_tiny], writes=[r_tiny])
            P.op("dve", I("reduce_sum", out=tiny[:, 51:52], in_=lt[:], axis=mybir.AxisListType.X), reads=[r_tiny], writes=[r_tiny])
            P.op("act", I("activation", out=tiny[:, 52:54], in_=tiny[:, 50:52], func=AF.Exp), reads=[r_tiny], writes=[r_tiny])
            P.op("dve", I("scalar_tensor_tensor", out=sc[:, 4 * l: 4 * l + 1], in0=tiny[:, 53:54], scalar=-LAM_INIT[l], in1=tiny[:, 52:53], op0=ALU.add, op1=ALU.subtract),
                 reads=[r_tiny], writes=[r_const])
            P.op("dve", I("tensor_scalar", out=sc[:, 4 * l + 2: 4 * l + 3], in0=sgt[:, l: l + 1], scalar1=0.5 * (1.0 - LAM_INIT[l]), scalar2=None, op0=ALU.mult),
                 reads=[r_const], writes=[r_const])

        def proj_fm(wslot, r_w, g, tb, bk, nk=16, src_row0=0):
            for kc in range(nk):
                P.op("pe", I("matmul", bank(bk), lhsT=wslot[:, kc * 128:(kc + 1) * 128],
                                                     rhs=row(g, src_row0 + kc, tb * 512, tb * 512 + 512),
                                                     start=(kc == 0), stop=(kc == nk - 1)),
                     reads=[r_w, r_reg[g][src_row0 + kc]], writes=[r_bank[bk]], mark=(kc == nk - 1))

        def make_xT(g, tt, xn_ap, r_xn, bk0=6):
            for half in range(2):
                bk = bk0 + half
                for i in range(8):
                    kc = half * 8 + i
                    P.op("pe", I("transpose", out=bank_bf(bk)[:, i * 128:(i + 1) * 128], in_=xn_ap[:, kc * 128:(kc + 1) * 128], identity=ident[:]),
                         reads=[r_xn, r_const], writes=[r_bank[bk]], mark=(i == 7))
                dst = REG[g][:].rearrange("p (k t) -> p k t", k=16)[:, half * 8: half * 8 + 8, tt * 128:(tt + 1) * 128]
                src = bank_bf(bk).rearrange("p (k t) -> p k t", k=8)
                P.op("dve", I("tensor_copy", out=dst, in_=src),
                     reads=[r_bank[bk]], writes=[r_reg[g][half * 8 + i] for i in range(8)])

        XY = [av(0, 2048, F32), av(8192, 2048, F32), av(36864, 2048, F32)]
        XN = av(16384, 2048, BF16)
        LNG = av(20480, 2048, F32)
        LNB = av(28672, 2048, F32)

        r_xy = [Res("xy0"), Res("xy1")]
        r_xn = Res("xn")
        for tt in range(16):
            b = tt % 2
            P.dma("sp", I("dma_start", out=XY[b], in_=x_d[tt * 128:(tt + 1) * 128, :]), writes=[r_xy[b]], semres=r_xy[b])
            P.op("act", I("activation", out=XN, in_=XY[b], func=AF.Copy), reads=[r_xy[b]], writes=[r_xn])
            make_xT(0, tt, XN, r_xn)
        issue_upto(LOOKAHEAD - 1)
        P.barrier()
        dump("xT0", REG[0][:], [128, 16 * 2048], BF16, r_reg[0])
        if stop_after == "pre":
            finish()
            return nc

        for l in range(L):
            gx = l % 2
            ga_ = 1 - gx
            vb = l * 64
            db = l * 40
            nlam = sc[:, 4 * l: 4 * l + 1]
            cgain = sc[:, 4 * l + 2: 4 * l + 3]
            for a in range(2):
                P.dma("pool", I("dma_start", out=wgt[:, a * 1024:(a + 1) * 1024], in_=wg_d[l * 2 + a], max_dma_last_dim=4096),
                      writes=[r_wgt], semres=r_wgt)

            VW = 130
            BS = []
            for bi in range(2):
                o = bi * 16448
                vv = av(o + 8192, 16 * VW, BF16)
                BS.append(dict(QT=av(o, 2048, BF16), r_qt=Res("qt%d" % bi), KT=av(o + 4096, 2048, BF16), r_kt=Res("kt%d" % bi),
                               VV=vv.rearrange("p (t w) -> p t w", t=16), r_vv=Res("vv%d" % bi),
                               GAS=av(o + 12352, 2048, BF16), r_gas=Res("gas%d" % bi)))
                P.op("dve", I("memset", BS[bi]["VV"][:, :, 128:130], 1.0), writes=[BS[bi]["r_vv"]])
            PT = [av(32896 + i * 1024, 512, BF16) for i in range(4)]; r_pt = [Res("pt%d" % i) for i in range(4)]
            U3 = av(36992, 512, F32).rearrange("p (s e) -> p s e", s=4); r_u = Res("u")
            ATM3 = av(39040, 512, BF16).rearrange("p (s e) -> p s e", s=4); r_atm = Res("atm")
            JUNK = av(40064, 128, BF16); r_junk = Res("junk")
            TG = av(40320, 512, F32); r_tg = Res("tg")
            r_nrm = Res("nrm")
            OV = [pm[2 + c][:].rearrange("p (s w) -> p s w", s=4) for c in range(2)]
            pctr = {"proj": 0, "step": 0, "prologue": True}

            def proj_units(h):
                bs = BS[h % 2]
                units = []

                def u_fm(kind, tb, dst, r_dst, use_act):
                    def f():
                        w, r_w = get_w(J[(l, kind, h)])
                        bk = (3, 0, 1, 2)[pctr["proj"] % 4] if pctr["prologue"] else 3
                        pctr["proj"] += 1
                        proj_fm(w, r_w, gx, tb, bk)
                        if kind == "ga":
                            P.op("act", I("activation", out=TG, in_=bank(bk), func=AF.Tanh, scale=0.5), reads=[r_bank[bk]], writes=[r_tg])
                            P.op("dve", I("scalar_tensor_tensor", out=dst[:, tb * 512:(tb + 1) * 512], in0=TG, scalar=1.0, in1=bank(bk), op0=ALU.add, op1=ALU.mult),
                                 reads=[r_tg, r_bank[bk]], writes=[r_dst])
                        elif use_act:
                            P.op("act", I("activation", out=dst[:, tb * 512:(tb + 1) * 512], in_=bank(bk), func=AF.Copy), reads=[r_bank[bk]], writes=[r_dst])
                        else:
                            P.op("dve", I("tensor_copy", out=dst[:, tb * 512:(tb + 1) * 512], in_=bank(bk)), reads=[r_bank[bk]], writes=[r_dst])
                    return f

                def u_v(g4):
                    def f():
                        w, r_w = get_w(J[(l, "v", h)])
                        bk = (3, 0, 1, 2)[pctr["proj"] % 4] if pctr["prologue"] else 3
                        pctr["proj"] += 1
                        for t4 in range(4):
                            tt = g4 * 4 + t4
                            c0 = t4 * 128
                            for kc in range(16):
                                P.op("pe", I("matmul", bank(bk, c0, c0 + 128), lhsT=row(gx, kc, tt * 128, tt * 128 + 128),
                                             rhs=w[:, kc * 128:(kc + 1) * 128], start=(kc == 0), stop=(kc == 15)),
                                     reads=[r_w, r_reg[gx][kc]], writes=[r_bank[bk]], mark=(kc == 15))
                        P.op("dve", I("tensor_copy", out=bs["VV"][:, g4 * 4:(g4 + 1) * 4, 0:128], in_=bank(bk).rearrange("p (t e) -> p t e", t=4)),
                             reads=[r_bank[bk]], writes=[bs["r_vv"]])
                    return f

                for tb in range(4):
                    units.append(u_fm("q", tb, bs["QT"], bs["r_qt"], False))
                for tb in range(4):
                    units.append(u_fm("k", tb, bs["KT"], bs["r_kt"], True))
                for g4 in range(4):
                    units.append(u_v(g4))
                for tb in range(4):
                    units.append(u_fm("ga", tb, bs["GAS"], bs["r_gas"], False))
                return units

            def group_steps(h, G0):
                W = HEAD_W[h]
                slope = SLOPES[h]
                steps = []
                for qs in range(G0, G0 + 512, W):
                    kt_hi = (qs + W) // 128 - 1
                    kt_lo = 0
                    while kt_lo < kt_hi and slope * (qs - (kt_lo * 128 + 127)) >= 128.0:
                        kt_lo += 1
                    for kt in range(kt_lo, kt_hi + 1):
                        for c in range(2):
                            steps.append(dict(qs=qs, W=W, kt=kt, c0=max(0, kt * 128 - qs), diag=(kt * 128 >= qs), c=c, sub0=(qs - G0) // 128))
                return steps

            pend = {"job": None, "budget": 0.0}

            def flush_pending():
                if pend["job"] is not None:
                    normalise_b(*pend["job"])
                    pend["job"] = None

            def spend(us):
                if pend["job"] is not None:
                    pend["budget"] -= us
                    if pend["budget"] <= 0:
                        flush_pending()

            def normalise_a(h, G0):
                rb = [r_bank[4], r_bank[5], r_bank[6], r_bank[7]]
                P.op("dve", I("reciprocal", out=nrm[:, 0:4], in_=OV[0][:, :, 128]), reads=rb[0:2], writes=[r_nrm])
                P.op("dve", I("reciprocal", out=nrm[:, 4:8], in_=OV[1][:, :, 128]), reads=rb[2:4], writes=[r_nrm])
                P.op("dve", I("tensor_scalar", out=nrm[:, 4:8], in0=nrm[:, 4:8], scalar1=nlam, scalar2=None, op0=ALU.mult), reads=[r_nrm, r_const], writes=[r_nrm])
                for sub in range(4):
                    P.op("dve", I("tensor_scalar", out=U3[:, sub, :], in0=OV[0][:, sub, 0:128], scalar1=nrm[:, sub:sub + 1], scalar2=None, op0=ALU.mult),
                         reads=rb[0:2] + [r_nrm], writes=[r_u])
                    P.op("dve", I("scalar_tensor_tensor", out=U3[:, sub, :], in0=OV[1][:, sub, 0:128], scalar=nrm[:, 4 + sub:5 + sub], in1=U3[:, sub, :], op0=ALU.mult, op1=ALU.add),
                         reads=rb[2:4] + [r_nrm, r_u], writes=[r_u])
                for sub in range(4):
                    P.op("act", I("activation", out=JUNK, in_=U3[:, sub, :], func=AF.Square, accum_out=nrm[:, 8 + sub:9 + sub]),
                         reads=[r_u], writes=[r_junk, r_nrm])
                P.op("dve", I("tensor_scalar", out=nrm[:, 12:16], in0=nrm[:, 8:12], scalar1=1.0 / 128.0, scalar2=SUBLN_EPS, op0=ALU.mult, op1=ALU.add),
                     reads=[r_nrm], writes=[r_nrm])
                P.op("pool", I("tensor_tensor", out=nrm[:, 12:16], in0=nrm[:, 12:16], in1=sc[:, 20:24], op=ALU.pow), reads=[r_nrm, r_const], writes=[r_nrm])
                for sub in range(4):
                    P.op("dve", I("tensor_scalar", out=ATM3[:, sub, :], in0=U3[:, sub, :], scalar1=nrm[:, 12 + sub:13 + sub], scalar2=None, op0=ALU.mult),
                         reads=[r_u, r_nrm], writes=[r_atm])

            def normalise_b(h, G0):
                bs = BS[h % 2]
                for sub in range(4):
                    P.op("pe", I("transpose", out=bank_bf(3)[:, sub * 128:(sub + 1) * 128], in_=ATM3[:, sub, :], identity=ident[:]),
                         reads=[r_atm, r_const], writes=[r_bank[3]], mark=(sub == 3))
                P.op("dve", I("scalar_tensor_tensor", out=row(ga_, h, G0, G0 + 512), in0=bank_bf(3)[:, 0:512], scalar=cgain, in1=bs["GAS"][:, G0:G0 + 512],
                              op0=ALU.mult, op1=ALU.mult),
                     reads=[r_bank[3], bs["r_gas"], r_const], writes=[r_reg[ga_][h]])

            def attention(h, units):
                bs = BS[h % 2]
                nu = len(units)
                ust = {"done": 0}
                groups = [group_steps(h, G0) for G0 in range(0, S, 512)]
                n_total = sum(len(g) for g in groups)
                flat = [(gi, st_) for gi, g in enumerate(groups) for st_ in g]

                def emit_qk(idx):
                    stp = flat[idx][1]
                    sb_ = pctr["step"] % 3
                    stp["sbank"] = sb_
                    stp["pt"] = pctr["step"] % 4
                    pctr["step"] += 1
                    c, kt, c0, qs, W = stp["c"], stp["kt"], stp["c0"], stp["qs"], stp["W"]
                    P.op("pe", I("matmul", bank(sb_, c0, W), lhsT=bs["KT"][c * 64:(c + 1) * 64, kt * 128:(kt + 1) * 128],
                                 rhs=bs["QT"][c * 64:(c + 1) * 64, qs + c0: qs + W], start=True, stop=True),
                         reads=[bs["r_kt"], bs["r_qt"]], writes=[r_bank[sb_]])

                def emit_units(upto):
                    while ust["done"] < min(upto, nu):
                        if ust["done"] >= 12:
                            flush_pending()
                        units[ust["done"]]()
                        ust["done"] += 1
                        spend(3.7)

                emit_qk(0)
                if n_total > 1:
                    emit_qk(1)
                opened = -1
                for i in range(n_total):
                    gi, stp = flat[i]
                    c, kt, c0, qs, W = stp["c"], stp["kt"], stp["c0"], stp["qs"], stp["W"]
                    sb_, pi = stp["sbank"], stp["pt"]
                    oi = (kt * 128 - qs) // 128 + 15
                    bias_ap = abias[:, h * NOFF + oi: h * NOFF + oi + 1]
                    P.op("act", I("activation", out=PT[pi][:, c0:W], in_=bank(sb_, c0, W), func=AF.Exp, scale=0.125, bias=bias_ap),
                         reads=[r_bank[sb_], r_const], writes=[r_pt[pi]])
                    if stp["diag"]:
                        P.op("dve", I("tensor_tensor", out=PT[pi][:, c0:c0 + 128], in0=PT[pi][:, c0:c0 + 128], in1=tri[:, 0:128], op=ALU.mult),
                             reads=[r_pt[pi], r_const], writes=[r_pt[pi]])
                    if i + 2 < n_total:
                        emit_qk(i + 2)
                    if opened != gi:
                        for b_ in range(4, 8):
                            P.op("pe", I("matmul", bank(b_), lhsT=zeros_bf[:], rhs=ones_bf[:], start=True, stop=True, skip_group_check=True),
                                 reads=[r_const], writes=[r_bank[b_]], mark=(b_ == 7))
                        opened = gi
                    subs = list(range(c0 // 128, W // 128))
                    for j in subs:
                        gsub = stp["sub0"] + j
                        P.op("pe", I("matmul", OV[c][:, gsub, 0:129], lhsT=PT[pi][:, j * 128:(j + 1) * 128], rhs=bs["VV"][:, kt, 0:129],
                                     start=False, stop=False, skip_group_check=True),
                             reads=[bs["r_vv"], r_pt[pi]], writes=[r_bank[4 + 2 * c + gsub // 2]], mark=(j == subs[-1]))
                    last_of_group = (i + 1 == n_total) or (flat[i + 1][0] != gi)
                    spend(0.25 * (W - c0) / 512.0 + 0.08 * len(subs) + 0.1)
                    if last_of_group:
                        flush_pending()
                        normalise_a(h, gi * 512)
                        pend["job"] = (h, gi * 512)
                        pend["budget"] = 13.0
                        emit_units(ust["done"] + 1)
                    emit_units(((i + 1) * nu) // n_total)
                emit_units(nu)

            for u in proj_units(0):
                u()
            pctr["prologue"] = False
            for h in range(NH):
                attention(h, proj_units(h + 1) if h + 1 < NH else [])
            flush_pending()
            issue_upto(J[(l, "xb", 0)] + LOOKAHEAD - 1)
            P.barrier()
            dump("att%d" % l, REG[ga_][:, 0:8 * 2048], [128, 8 * 2048], BF16, r_reg[ga_])
            if stop_after == "p1_%d" % l:
                finish()
                return nc

            XB = av(0, 2052, F32); r_xb = Res("xb")
            XC = av(8208, 2048, F32); r_xc = Res("xc")
            XCB = av(16400, 2048, BF16); r_xcb = Res("xcb")
            GBS = av(20496, 2048, BF16); r_gbs = Res("gbs")
            SQ = av(24592, 1024, F32); r_sq = Res("sq")
            T2 = av(28688, 1024, F32); r_t2 = Res("t2")
            HH = av(32784, 1024, F32); r_hh = Res("hh")
            TT = av(36880, 512, F32); r_ttt = Res("ttt")
            P.op("dve", I("memset", XB[:, 0:4], 0.0), writes=[r_xb])
            for n in range(8):
                wx, r_wx = get_w(J[(l, "xb", n)])
                for tb in range(4):
                    proj_fm(wx, r_wx, gx, tb, tb)
                    P.op("dve", I("tensor_copy", out=XB[:, 4 + tb * 512: 4 + (tb + 1) * 512], in_=bank(tb)), reads=[r_bank[tb]], writes=[r_xb])
                wgb, r_wgb = get_w(J[(l, "gb", n)])
                for tb in range(4):
                    proj_fm(wgb, r_wgb, gx, tb, 4 + tb)
                    P.op("act", I("activation", out=TT, in_=bank(4 + tb), func=AF.Tanh, scale=0.5), reads=[r_bank[4 + tb]], writes=[r_ttt])
                    P.op("dve", I("scalar_tensor_tensor", out=GBS[:, tb * 512:(tb + 1) * 512], in0=TT, scalar=1.0, in1=bank(4 + tb), op0=ALU.add, op1=ALU.mult),
                         reads=[r_ttt, r_bank[4 + tb]], writes=[r_gbs])
                cw = [vecs[:, vb + j * 8 + n: vb + j * 8 + n + 1] for j in range(4)]
                cb = vecs[:, vb + 32 + n: vb + 33 + n]
                P.op("dve", I("tensor_scalar", out=XC, in0=XB[:, 1:2049], scalar1=cw[0], scalar2=cb, op0=ALU.mult, op1=ALU.add),
                     reads=[r_xb, r_const], writes=[r_xc])
                for j in range(1, 4):
                    P.op("dve", I("scalar_tensor_tensor", out=XC, in0=XB[:, 1 + j: 2049 + j], scalar=cw[j], in1=XC, op0=ALU.mult, op1=ALU.add),
                         reads=[r_xb, r_xc, r_const], writes=[r_xc])
                P.op("act", I("activation", out=XCB, in_=XC, func=AF.Copy), reads=[r_xc], writes=[r_xcb])
                m4sp = drv[:, db + n: db + n + 1]
                p4sp = drv[:, db + 8 + n: db + 9 + n]
                hbr = drv[:, db + 16 + n: db + 17 + n]
                hbi = drv[:, db + 24 + n: db + 25 + n]
                T1 = XB[:, 4:1028]
                AA = XB[:, 1028:2052]
                for hf in range(2):
                    tk0 = hf * 1024
                    for gi in range(2):
                        for sub in range(2):
                            bk = gi * 2 + sub
                            P.op("pe", I("matmul", bank(bk), lhsT=wgt[:, gi * 1024 + n * 128: gi * 1024 + (n + 1) * 128],
                                rhs=XCB[:, tk0 + sub * 512: tk0 + (sub + 1) * 512], start=True, stop=True),
                                reads=[r_wgt, r_xcb], writes=[r_bank[bk]])
                    P.op("act", I("activation", out=T1, in_=pm[0][:], func=AF.Tanh, scale=0.5, bias=hbr),
                         reads=[r_bank[0], r_bank[1], r_const, r_xc], writes=[r_xb])
                    P.op("act", I("activation", out=AA, in_=T1, func=AF.Exp, scale=m4sp, bias=m4sp), reads=[r_xb, r_const], writes=[r_xb])
                    P.op("act", I("activation", out=T1, in_=T1, func=AF.Tanh, scale=p4sp, bias=p4sp), reads=[r_xb, r_const], writes=[r_xb])
                    P.op("act", I("activation", out=T2, in_=pm[1][:], func=AF.Tanh, scale=0.5, bias=hbi),
                         reads=[r_bank[2], r_bank[3], r_const], writes=[r_t2])
                    P.op("dve", I("tensor_tensor", out=SQ, in0=AA, in1=AA, op=ALU.mult), reads=[r_xb], writes=[r_sq])
                    P.op("dve", I("scalar_tensor_tensor", out=SQ, in0=SQ, scalar=1.0, in1=T1, op0=ALU.add, op1=ALU.mult), reads=[r_sq, r_xb], writes=[r_sq])
                    P.op("act", I("activation", out=SQ, in_=SQ, func=AF.Sqrt), reads=[r_sq], writes=[r_sq])
                    P.op("dve", I("scalar_tensor_tensor", out=T2, in0=T2, scalar=1.0, in1=XC[:, tk0: tk0 + 1024], op0=ALU.add, op1=ALU.mult),
                         reads=[r_t2, r_xc], writes=[r_t2])
                    P.op("dve", I("scalar_tensor_tensor", out=T2, in0=T2, scalar=0.5, in1=SQ, op0=ALU.mult, op1=ALU.mult), reads=[r_t2, r_sq], writes=[r_t2])
                    if hf == 0:
                        P.op("dve", I("tensor_tensor_scan", out=HH, data0=AA, data1=T2, initial=0.0, op0=ALU.mult, op1=ALU.add),
                             reads=[r_xb, r_t2], writes=[r_hh])
                    else:
                        P.op("dve", I("tensor_copy", out=tiny[:, 60:61], in_=HH[:, 1023:1024]), reads=[r_hh], writes=[r_tiny])
                        P.op("dve", I("tensor_tensor_scan", out=HH, data0=AA, data1=T2, initial=tiny[:, 60:61], op0=ALU.mult, op1=ALU.add),
                             reads=[r_xb, r_t2, r_tiny], writes=[r_hh])
                    P.op("dve", I("scalar_tensor_tensor", out=row(ga_, 8 + n, tk0, tk0 + 1024), in0=HH, scalar=0.5, in1=GBS[:, tk0: tk0 + 1024], op0=ALU.mult, op1=ALU.mult),
                         reads=[r_hh, r_gbs], writes=[r_reg[ga_][8 + n]])
            issue_upto(J[(l, "ma", 0)] + LOOKAHEAD - 1)
            P.barrier()
            dump("rec%d" % l, REG[ga_][:, 8 * 2048:16 * 2048], [128, 8 * 2048], BF16, r_reg[ga_])
            if stop_after == "p2_%d" % l:
                finish()
                return nc

            TA = [av(0, 512, F32), av(2048, 512, F32)]; r_ta = [Res("ta0"), Res("ta1")]
            TB = [av(4096, 512, F32), av(6144, 512, F32)]; r_tb = [Res("tb0"), Res("tb1")]
            MG = [av(8192, 2048, BF16), av(12288, 2048, BF16)]; r_mg = [Res("mg0"), Res("mg1")]
            for dc in range(16):
                wa, r_wa = get_w(J[(l, "ma", dc)])
                wb, r_wb = get_w(J[(l, "mb", dc)])
                wpab, r_wpab = get_w(J[(l, "pab", dc)])
                mb_ = dc % 2
                for tb in range(4):
                    s_ = tb % 2
                    b0 = 4 * s_
                    proj_fm(wa, r_wa, gx, tb, b0 + 0)
                    proj_fm(wb, r_wb, gx, tb, b0 + 1)
                    proj_fm(wpab[:, 0:1024], r_wpab, ga_, tb, b0 + 2, nk=8, src_row0=0)
                    proj_fm(wpab[:, 1024:2048], r_wpab, ga_, tb, b0 + 3, nk=8, src_row0=8)
                    P.op("act", I("activation", out=TA[s_], in_=bank(b0), func=AF.Tanh, scale=0.5), reads=[r_bank[b0]], writes=[r_ta[s_]])
                    P.op("act", I("activation", out=TB[s_], in_=bank(b0 + 1), func=AF.Tanh, scale=0.5), reads=[r_bank[b0 + 1]], writes=[r_tb[s_]])
                    P.op("dve", I("scalar_tensor_tensor", out=TA[s_], in0=TA[s_], scalar=1.0, in1=bank(b0 + 2), op0=ALU.add, op1=ALU.mult),
                         reads=[r_ta[s_], r_bank[b0 + 2]], writes=[r_ta[s_]])
                    P.op("dve", I("scalar_tensor_tensor", out=TB[s_], in0=TB[s_], scalar=1.0, in1=bank(b0 + 3), op0=ALU.add, op1=ALU.mult),
                         reads=[r_tb[s_], r_bank[b0 + 3]], writes=[r_tb[s_]])
                    P.op("dve", I("tensor_tensor", out=MG[mb_][:, tb * 512:(tb + 1) * 512], in0=TA[s_], in1=TB[s_], op=ALU.add),
                         reads=[r_ta[s_], r_tb[s_]], writes=[r_mg[mb_]])
                dst = mrg_d.rearrange("t p d j -> p t d j")[:, :, dc, :]
                src = MG[mb_].rearrange("p (t j) -> p t j", t=16)
                P.dma("sp", I("dma_start", out=dst, in_=src), reads=[r_mg[mb_]], writes=[r_mrg], semres=r_mg[mb_])
            P.barrier()
            if stop_after == "p3_%d" % l:
                finish()
                return nc

            state["gate_open"].add(("p4", l))
            gw = gx
            for dc in range(16):
                P.dma("pool", I("dma_start", out=row(gw, dc), in_=wo_d[l * 16 + dc], max_dma_last_dim=8192),
                      writes=[r_reg[gw][dc]], semres=r_reg[gw][dc])
            r_ln = Res("ln")
            P.dma("sp", I("dma_start", out=LNG, in_=lnp_d[l * 2 + 0]), writes=[r_ln], semres=r_ln)
            P.dma("sp", I("dma_start", out=LNB, in_=lnp_d[l * 2 + 1]), writes=[r_ln], semres=r_ln)
            r_xy = [Res("xy0"), Res("xy1"), Res("xy2")]
            r_xn = Res("xn")
            src_x = x_d if l == 0 else x1_d
            dst_x = x1_d if l == 0 else out_d
            pending_T = None
            for tt in range(16):
                b = tt % 3
                mgw, r_mgw = get_w(J[(l, "mg", tt)])
                P.dma("sp", I("dma_start", out=XY[b], in_=src_x[tt * 128:(tt + 1) * 128, :]),
                      reads=([r_x1] if l == 1 else []), writes=[r_xy[b]], semres=r_xy[b])
                bset = 4 * (tt % 2)
                for eb in range(4):
                    bk = bset + eb
                    for dc in range(16):
                        P.op("pe", I("matmul", bank(bk), lhsT=mgw[:, dc * 128:(dc + 1) * 128],
                                                                                 rhs=row(gw, dc, eb * 512, eb * 512 + 512), start=(dc == 0), stop=(dc == 15)),
                             reads=[r_mgw, r_reg[gw][dc]], writes=[r_bank[bk]], mark=(dc == 15))
                    P.op("dve", I("scalar_tensor_tensor", out=XY[b][:, eb * 512:(eb + 1) * 512], in0=bank(bk), scalar=C_OUT,
                                                                                 in1=XY[b][:, eb * 512:(eb + 1) * 512], op0=ALU.mult, op1=ALU.add),
                         reads=[r_bank[bk], r_xy[b]], writes=[r_xy[b]])
                    P.op("dve", I("bn_stats", out=tiny[:, eb * 6:(eb + 1) * 6], in_=XY[b][:, eb * 512:(eb + 1) * 512]), reads=[r_xy[b]], writes=[r_tiny])
                    if eb == 3 and pending_T is not None:
                        make_xT(ga_, pending_T[0], XN, r_xn, bk0=pending_T[1])
                        pending_T = None
                P.op("dve", I("bn_aggr", out=tiny[:, 24:26], in_=tiny[:, 0:24]), reads=[r_tiny], writes=[r_tiny])
                P.op("act", I("activation", out=tiny[:, 26:27], in_=tiny[:, 25:26], func=AF.Sqrt, scale=1.0, bias=sc[:, 17:18]), reads=[r_tiny, r_const], writes=[r_tiny])
                P.op("dve", I("reciprocal", out=tiny[:, 27:28], in_=tiny[:, 26:27]), reads=[r_tiny], writes=[r_tiny])
                P.op("dve", I("scalar_tensor_tensor", out=tiny[:, 28:29], in0=tiny[:, 24:25], scalar=-1.0, in1=tiny[:, 27:28], op0=ALU.mult, op1=ALU.mult),
                     reads=[r_tiny], writes=[r_tiny])
                P.op("act", I("activation", out=XY[b], in_=XY[b], func=AF.Identity, scale=tiny[:, 27:28], bias=tiny[:, 28:29]),
                     reads=[r_xy[b], r_tiny], writes=[r_xy[b]])
                P.op("dve", I("tensor_tensor", out=XY[b], in0=XY[b], in1=LNG, op=ALU.mult), reads=[r_xy[b], r_ln], writes=[r_xy[b]])
                P.op("pool", I("tensor_tensor", out=XY[b], in0=XY[b], in1=LNB, op=ALU.add), reads=[r_xy[b], r_ln], writes=[r_xy[b]])
                P.dma("sp", I("dma_start", out=dst_x[tt * 128:(tt + 1) * 128, :], in_=XY[b]),
                      reads=[r_xy[b]], writes=([r_x1] if l == 0 else []), semres=r_xy[b])
                if l < L - 1:
                    P.op("act", I("activation", out=XN, in_=XY[b], func=AF.Copy), reads=[r_xy[b]], writes=[r_xn])
                    pending_T = (tt, bset)
            if pending_T is not None:
                make_xT(ga_, pending_T[0], XN, r_xn, bk0=pending_T[1])
                pending_T = None
            if l < L - 1:
                issue_upto(J[(l + 1, "q", 0)] + LOOKAHEAD - 1)
            P.barrier()
            if stop_after == "p4_%d" % l:
                finish()
                return nc

        P.emit()
    return nc


_NC_CACHE = {}


def _prep_shared(w_in, conv_w, conv_b, w_rgate, b_rgate, w_igate, b_igate, lru_lambda,
                 lam_q1, lam_k1, lam_q2, lam_k2, subln_g, w_pa, w_pb, w_out, ln_g, ln_b):
    f = np.float32
    w1 = np.ascontiguousarray(w_in.astype(f, copy=False).reshape(L, 16, 128, 80, 128).transpose(0, 3, 2, 1, 4)).reshape(L * 80, 128, 2048)
    wp = np.stack([w_pa, w_pb], axis=1).astype(f, copy=False)
    wp = np.ascontiguousarray(wp.reshape(L, 2, 8, 128, 16, 128).transpose(0, 1, 4, 3, 2, 5)).reshape(L * 2 * 16, 128, 1024)
    wo = np.ascontiguousarray(w_out.astype(f, copy=False).reshape(L * 16, 128, 2048))
    wg = np.stack([w_rgate, w_igate], axis=1).astype(f, copy=False)
    wg = np.ascontiguousarray(wg.transpose(0, 1, 3, 2, 4)).reshape(L * 2, 128, 1024)
    vs = np.concatenate([conv_w.astype(f, copy=False),
                         conv_b[:, None, :], b_rgate[:, None, :], b_igate[:, None, :], lru_lambda[:, None, :]], axis=1)
    vecs = np.ascontiguousarray(vs.reshape(L, 8, 8, 128).transpose(3, 0, 1, 2)).reshape(128, L * 64).astype(f)
    sg = np.ascontiguousarray(subln_g.astype(f, copy=False).T)
    lnp = np.stack([ln_g, ln_b], axis=1).astype(f, copy=False)
    lnp = np.ascontiguousarray(np.broadcast_to(lnp[:, :, None, :], (L, 2, 128, D))).reshape(L * 2, 128, D)
    lv = np.stack([lam_q1, lam_q2, lam_k1, lam_k2], axis=1).astype(f, copy=False).reshape(1, L * 256)
    lamv = np.ascontiguousarray(np.broadcast_to(lv, (128, L * 256)))
    ident = np.eye(128, dtype=np.float32).astype(ml_dtypes.bfloat16)
    ki = np.arange(128)
    tri1 = (ki[None, :] >= ki[:, None]).astype(np.float32)
    tri = np.concatenate([tri1, tri1], axis=1).astype(ml_dtypes.bfloat16)
    offs = np.arange(NOFF) - 15
    ab = np.zeros((128, NH, NOFF), np.float32)
    for h in range(NH):
        ab[:, h, :] = SLOPES[h] * (ki[:, None] + 128.0 * offs[None, :])
    abias = ab.reshape(128, NH * NOFF)
    return dict(w1=w1, wp=wp, wo=wo, wg=wg, vecs=vecs, sg=sg, lnp=lnp, lamv=lamv, ident=ident, tri=tri, abias=abias)


def kernel(x, w_in, conv_w, conv_b, w_rgate, b_rgate, w_igate, b_igate, lru_lambda,
           lam_q1, lam_k1, lam_q2, lam_k2, subln_g, w_pa, w_pb, w_out, ln_g, ln_b):
    x = np.asarray(x, dtype=np.float32)
    args = [np.asarray(a, dtype=np.float32) for a in (w_in, conv_w, conv_b, w_rgate, b_rgate, w_igate, b_igate, lru_lambda,
                                                      lam_q1, lam_k1, lam_q2, lam_k2, subln_g, w_pa, w_pb, w_out, ln_g, ln_b)]
    shared = _prep_shared(*args)
    if "nc" not in _NC_CACHE:
        _NC_CACHE["nc"] = build_program()
    nc = _NC_CACHE["nc"]
    n = x.shape[0]
    in_maps = []
    for b in range(n):
        m = dict(shared)
        m["x"] = np.ascontiguousarray(x[b])
        in_maps.append(m)
    res = run_bass_kernel_spmd(nc, in_maps, core_ids=list(range(n)))
    return np.stack([np.asarray(r["out"], dtype=np.float32) for r in res.results], axis=0)
```
